# Optimizing a Trainium2 kernel written in Bass

```python
import jax, jax.numpy as jnp
from jax import lax
import numpy as np

D_MODEL = 1024
BATCH = 2
SEQ = 16384
DEPTH = 2

D_MIX = D_MODEL
N_GROUPS = 4
D_GROUP = D_MIX // N_GROUPS
HEAD_DIM = 64
N_HEADS_GROUP = D_GROUP // HEAD_DIM
CHUNK = 128
Q_BLOCK = 128
SHORT_CONV_W = 3
CONF_CONV_W = 31
D_FF = 4 * D_MODEL
EPS = 1e-6
D_IN_PROJ = 10 * D_GROUP
SPLIT_POINTS = (2 * D_GROUP, 5 * D_GROUP, 8 * D_GROUP)

kernel_name = 'hymba_parallel_gmlp_shortconv_stickbreak_conformer'


def rms_norm(x, g=None):
    xf = x.astype(jnp.float32)
    y = xf * lax.rsqrt(jnp.mean(xf * xf, axis=-1, keepdims=True) + EPS)
    if g is not None:
        y = y * g.astype(jnp.float32)
    return y.astype(x.dtype)


def layer_norm(x, g, b):
    xf = x.astype(jnp.float32)
    xc = xf - jnp.mean(xf, axis=-1, keepdims=True)
    var = jnp.mean(xc * xc, axis=-1, keepdims=True)
    y = xc * lax.rsqrt(var + EPS) * g.astype(jnp.float32) + b.astype(jnp.float32)
    return y.astype(x.dtype)


def causal_depthwise_conv(x, w):
    k_w, c = w.shape
    return lax.conv_general_dilated(
        x, w[:, None, :].astype(x.dtype), window_strides=(1,), padding=[(k_w - 1, 0)],
        dimension_numbers=('NWC', 'WIO', 'NWC'), feature_group_count=c)


def spatial_gating_mixer(z, v_gain, w_s, b_s):
    z = jax.nn.gelu(z)
    u, v = jnp.split(z, 2, axis=-1)
    v = rms_norm(v, v_gain)
    b, s, _ = v.shape
    v = v.reshape(b, s // CHUNK, CHUNK, N_HEADS_GROUP, HEAD_DIM)
    mask = jnp.tril(jnp.ones((CHUNK, CHUNK), dtype=bool))
    w = jnp.where(mask, w_s, jnp.zeros_like(w_s))
    f = jnp.einsum('hts,bnshd->bnthd', w, v) + b_s.T[:, :, None]
    return u * f.reshape(b, s, D_GROUP)


def short_conv_mixer(z, w_conv):
    gate_b, gate_c, h = jnp.split(z, 3, axis=-1)
    return gate_b * causal_depthwise_conv(gate_c * h, w_conv)


def stick_breaking_attention(z):
    q, k, v = jnp.split(z, 3, axis=-1)
    b, s, _ = q.shape
    q = q.reshape(b, s, N_HEADS_GROUP, HEAD_DIM)
    k = k.reshape(b, s, N_HEADS_GROUP, HEAD_DIM)
    v = v.reshape(b, s, N_HEADS_GROUP, HEAD_DIM)
    scale = HEAD_DIM ** -0.5
    n_blocks = s // Q_BLOCK
    q_blocks = q.reshape(b, n_blocks, Q_BLOCK, N_HEADS_GROUP, HEAD_DIM).transpose(1, 0, 2, 3, 4)
    key_pos = jnp.arange(s)

    def one_block(args):
        q_blk, blk_idx = args
        logits = jnp.einsum('bqhd,bkhd->bhqk', q_blk, k,
                            preferred_element_type=jnp.float32) * scale
        q_pos = blk_idx * Q_BLOCK + jnp.arange(Q_BLOCK)
        causal = key_pos[None, :] < q_pos[:, None]
        log_beta = jax.nn.log_sigmoid(logits)
        log_one_minus = jnp.where(causal, jax.nn.log_sigmoid(-logits), 0.0)
        log_stick = lax.cumsum(log_one_minus, axis=3, reverse=True) - log_one_minus
        weights = jnp.where(causal, jnp.exp(log_beta + log_stick), 0.0)
        return jnp.einsum('bhqk,bkhd->bqhd', weights.astype(v.dtype), v)

    out = lax.map(one_block, (q_blocks, jnp.arange(n_blocks)))
    return out.transpose(1, 0, 2, 3, 4).reshape(b, s, D_GROUP)


def conformer_conv_mixer(z, w_conv, ln_g, ln_b):
    a, g = jnp.split(z, 2, axis=-1)
    h = a * jax.nn.sigmoid(g)
    h = causal_depthwise_conv(h, w_conv)
    h = layer_norm(h, ln_g, ln_b)
    return jax.nn.silu(h)


def setup_inputs(seed: int = 0) -> dict:
    key = jax.random.key(seed)
    ks = jax.random.split(key, 17)
    nrm = jax.random.normal
    x = nrm(ks[0], (BATCH, SEQ, D_MODEL), jnp.float32)
    norm_mix_g = 1.0 + 0.05 * nrm(ks[1], (DEPTH, D_MODEL), jnp.float32)
    w_in = nrm(ks[2], (DEPTH, D_MODEL, D_IN_PROJ), jnp.float32) * D_MODEL ** -0.5
    gmlp_v_g = 1.0 + 0.05 * nrm(ks[3], (DEPTH, D_GROUP), jnp.float32)
    gmlp_w_s = nrm(ks[4], (DEPTH, N_HEADS_GROUP, CHUNK, CHUNK), jnp.float32) * CHUNK ** -0.5
    gmlp_b_s = 1.0 + 0.05 * nrm(ks[5], (DEPTH, N_HEADS_GROUP, CHUNK), jnp.float32)
    short_conv_w = nrm(ks[6], (DEPTH, SHORT_CONV_W, D_GROUP), jnp.float32) * SHORT_CONV_W ** -0.5
    conf_conv_w = nrm(ks[7], (DEPTH, CONF_CONV_W, D_GROUP), jnp.float32) * CONF_CONV_W ** -0.5
    conf_ln_g = 1.0 + 0.05 * nrm(ks[8], (DEPTH, D_GROUP), jnp.float32)
    conf_ln_b = 0.02 * nrm(ks[9], (DEPTH, D_GROUP), jnp.float32)
    mix_out_g = 1.0 + 0.05 * nrm(ks[10], (DEPTH, D_MIX), jnp.float32)
    w_out = nrm(ks[11], (DEPTH, D_MIX, D_MODEL), jnp.float32) * D_MIX ** -0.5
    norm_ffn_g = 1.0 + 0.05 * nrm(ks[12], (DEPTH, D_MODEL), jnp.float32)
    w_up = nrm(ks[13], (DEPTH, D_MODEL, D_FF), jnp.float32) * D_MODEL ** -0.5
    w_down = nrm(ks[14], (DEPTH, D_FF, D_MODEL), jnp.float32) * D_FF ** -0.5
    final_norm_g = 1.0 + 0.05 * nrm(ks[15], (D_MODEL,), jnp.float32)
    return {'x': x, 'norm_mix_g': norm_mix_g, 'w_in': w_in, 'gmlp_v_g': gmlp_v_g,
            'gmlp_w_s': gmlp_w_s, 'gmlp_b_s': gmlp_b_s, 'short_conv_w': short_conv_w,
            'conf_conv_w': conf_conv_w, 'conf_ln_g': conf_ln_g, 'conf_ln_b': conf_ln_b,
            'mix_out_g': mix_out_g, 'w_out': w_out, 'norm_ffn_g': norm_ffn_g,
            'w_up': w_up, 'w_down': w_down, 'final_norm_g': final_norm_g}


def reference(x, norm_mix_g, w_in, gmlp_v_g, gmlp_w_s, gmlp_b_s, short_conv_w, conf_conv_w,
              conf_ln_g, conf_ln_b, mix_out_g, w_out, norm_ffn_g, w_up, w_down, final_norm_g):
    for l in range(DEPTH):
        h = rms_norm(x, norm_mix_g[l])
        z = jnp.einsum('bsd,de->bse', h, w_in[l])
        z_a, z_b, z_c, z_d = jnp.split(z, SPLIT_POINTS, axis=-1)
        y_a = spatial_gating_mixer(z_a, gmlp_v_g[l], gmlp_w_s[l], gmlp_b_s[l])
        y_b = short_conv_mixer(z_b, short_conv_w[l])
        y_c = stick_breaking_attention(z_c)
        y_d = conformer_conv_mixer(z_d, conf_conv_w[l], conf_ln_g[l], conf_ln_b[l])
        y = jnp.concatenate([rms_norm(y_a), rms_norm(y_b), rms_norm(y_c), rms_norm(y_d)],
                            axis=-1) * mix_out_g[l]
        x = x + jnp.einsum('bse,ed->bsd', y, w_out[l])
        h = rms_norm(x, norm_ffn_g[l])
        a = jax.nn.relu(jnp.einsum('bsd,df->bsf', h, w_up[l]))
        x = x + jnp.einsum('bsf,fd->bsd', a * a, w_down[l])
    return rms_norm(x, final_norm_g)
```

```python
from contextlib import ExitStack
import numpy as np
import concourse.bass as bass
import concourse.mybir as mybir
from concourse.bass_utils import run_bass_kernel_spmd

F32 = mybir.dt.float32
BF16 = mybir.dt.bfloat16
AF = mybir.ActivationFunctionType
ALU = mybir.AluOpType
ET = mybir.EngineType

D = 1024
DG = 256
DFF = 4096
DIN = 2560
EPS = 1e-6
NEG = -240.0
PVL = 59
GROUPS = [[0, 1, 2, 3], [4, 5, 6, 7]]


class K:
    def __init__(self, nc):
        self.nc = nc
        self.eng = {'pe': nc.tensor, 'act': nc.scalar, 'dve': nc.vector, 'pool': nc.gpsimd, 'sp': nc.sync}
        self.sem = {}
        self.cnt = {}
        self.seen = {}
        self.lastw = {}
        self.readers = {}
        for e in ('pe', 'act', 'dve', 'pool'):
            self.sem[e] = nc.alloc_semaphore("s_" + e)
            self.cnt[e] = 0

    def _key(self, key):
        if key not in self.sem:
            self.sem[key] = self.nc.alloc_semaphore("d_" + str(key))
            self.cnt[key] = 0
        return self.sem[key]

    def wait(self, consumer, tok, skip_same=False):
        key, val = tok
        if consumer == 'pe' and key == 'pe':
            return
        if skip_same and key == consumer:
            return
        if key not in ('pe', 'act', 'dve', 'pool', 'cc'):
            val = self.cnt[key]
        if self.seen.get((consumer, key), 0) >= val:
            return
        self.seen[(consumer, key)] = val
        self.eng[consumer].wait_ge(self.sem[key], val)

    def _deps(self, e, reads, writes, skip_same=False):
        for b in reads:
            if b in self.lastw:
                self.wait(e, self.lastw[b], skip_same)
        for b in writes:
            if b in self.lastw:
                self.wait(e, self.lastw[b], skip_same)
            for t in self.readers.get(b, {}).items():
                self.wait(e, t, skip_same)

    def _commit(self, tok, reads, writes):
        key, val = tok
        for b in reads:
            r = self.readers.setdefault(b, {})
            if r.get(key, 0) < val:
                r[key] = val
        for b in writes:
            self.lastw[b] = tok
            self.readers[b] = {}

    def op(self, e, name, *args, r=(), w=(), same_ok=False, **kw):
        self._deps(e, r, w, skip_same=same_ok)
        ins = getattr(self.eng[e], name)(*args, **kw)
        self.cnt[e] += 1
        ins.then_inc(self.sem[e], 1)
        tok = (e, self.cnt[e])
        self._commit(tok, r, w)
        return tok

    def dma(self, q, out, in_, key, r=(), w=()):
        self._deps(q, r, w)
        sem = self._key(key)
        self.cnt[key] += 16
        self.eng[q].dma_start(out=out, in_=in_).then_inc(sem, 16)
        tok = (key, self.cnt[key])
        self._commit(tok, r, w)
        return tok

    def collective(self, kind, ins, outs, r=(), w=()):
        self._deps('pool', r, w)
        sem = self._key('cc')
        self.cnt['cc'] += 1
        self.nc.gpsimd.collective_compute(kind, ALU.bypass, replica_groups=GROUPS,
                                          ins=ins, outs=outs).then_inc(sem, 1)
        tok = ('cc', self.cnt['cc'])
        self._commit(tok, r, w)
        return tok

    def wait_keys(self, e, keys):
        for b in keys:
            if b in self.lastw:
                self.wait(e, self.lastw[b])

    def barrier(self, skip=()):
        for e in ('pe', 'act', 'dve', 'pool', 'sp'):
            for key, c in self.cnt.items():
                if c > 0 and key not in skip:
                    self.wait(e, (key, c))
        self.lastw = {}
        self.readers = {}


def build(S, L, dbg=None):
    TPC = S // 4
    NT = TPC // 512
    NQT = S // 512
    NKB = S // 128
    nc = bass.Bass("TRN2", target_bir_lowering=False)
    k = K(nc)
    NPV = PVL * L + 8

    xT_in = nc.dram_tensor("xT", [D, TPC], F32, kind="ExternalInput")
    w_in = nc.dram_tensor("w_in", [L, D, DIN], F32, kind="ExternalInput")
    w_out = nc.dram_tensor("w_out", [L, D, D], F32, kind="ExternalInput")
    w_up = nc.dram_tensor("w_up", [L, D, DFF], F32, kind="ExternalInput")
    w_down = nc.dram_tensor("w_down", [L, DFF, D], F32, kind="ExternalInput")
    pv_in = nc.dram_tensor("pv", [128, NPV], F32, kind="ExternalInput")
    vgb_in = nc.dram_tensor("vgb", [L, 128, DG], F32, kind="ExternalInput")
    wsT_in = nc.dram_tensor("wsT", [L, 128, 4, 128], F32, kind="ExternalInput")
    bsT_in = nc.dram_tensor("bsT", [L, 128, 2, 128], F32, kind="ExternalInput")
    outT = nc.dram_tensor("outT", [D, TPC], F32, kind="ExternalOutput")

    exchA = [nc.dram_tensor("exchA%d" % t, [4, 3, 64, 512], BF16) for t in range(NT)]
    exchB = [nc.dram_tensor("exchB%d" % t, [4, 2, 64, 512], BF16) for t in range(NT)]
    g1A = [nc.dram_tensor("g1A%d" % t, [4, 4, 3, 64, 512], BF16) for t in range(NT)]
    g1B = [nc.dram_tensor("g1B%d" % t, [4, 4, 2, 64, 512], BF16) for t in range(NT)]
    x2b = [nc.dram_tensor("x2b%d" % t, [3, 64, 4, 512], BF16) for t in range(NT)]
    g2 = [nc.dram_tensor("g2_%d" % t, [4, 3, 64, 4, 512], BF16) for t in range(NT)]
    loc = nc.dram_tensor("loc", [2 * DG, TPC], BF16)
    xa = nc.dram_tensor("xa", [D, TPC], F32)
    xb = nc.dram_tensor("xb", [D, TPC], F32)
    dbg_out = None
    if dbg == 'p2':
        dbg_out = tuple(nc.dram_tensor("dbg_x2b%d" % t, [3, 64, 4, 512], BF16, kind="ExternalOutput")
                        for t in range(NT))
    if dbg in ('p3a_ld', 'p3a_n', 'p3a_d', 'p3a_o'):
        dbg_out = (nc.dram_tensor("dbg_gin", [128, 6, 512], BF16, kind="ExternalOutput"),
                   nc.dram_tensor("dbg_ycat", [128, 8, 512], BF16, kind="ExternalOutput"))
    if dbg == 'p3a':
        dbg_out = (nc.dram_tensor("dbg_xa", [D, TPC], F32, kind="ExternalOutput"),)


    slot_sp = nc.partition_id([ET.SP]) % 4

    with ExitStack() as gs:
        uid = [0]

        def sb(name, shape, dt, st=gs):
            uid[0] += 1
            return st.enter_context(nc.sbuf_tensor("%s_%d" % (name, uid[0]), shape, dt))

        def ps(name, shape, dt, st=gs):
            return st.enter_context(nc.psum_tensor(name, shape, dt))

        pv = sb("pv_sb", [128, NPV], F32)
        ones = sb("ones", [128, 128], BF16)
        ident = sb("ident", [128, 128], BF16)
        epsc = sb("epsc", [128, 1], F32)
        onec = sb("onec", [128, 1], F32)
        k.dma('sp', pv[:], pv_in[:, :], 'ld_pv', w=['pv'])
        k.op('pool', 'memset', ones[:], 1.0, w=['ones'])
        k.op('pool', 'memset', epsc[:], EPS, w=['epsc'])
        k.op('pool', 'memset', onec[:], 1.0, w=['onec'])
        k.op('pool', 'affine_select', ident[:], ones[:], [[-1, 128]], ALU.is_equal, 0.0,
             base=0, channel_multiplier=1, r=['ones'], w=['ident'])
        pp = [ps("pp%d" % i, [128, 1024], F32) for i in range(4)]
        banks = [pp[i // 2][:, (i % 2) * 512:(i % 2 + 1) * 512] for i in range(8)]

        def rstd_from(psum_ap, n, out_ap, tmp_ap, rk, wk, tk):
            k.op('act', 'activation', tmp_ap, psum_ap, AF.Ln, bias=EPS, scale=1.0 / n, r=[rk], w=[tk])
            k.op('act', 'activation', out_ap, tmp_ap, AF.Exp, scale=-0.5, r=[tk], w=[wk])

        wl_n = [0]

        def load_weight_gen(stg, dst, src2d, nk, ncol, gcol, name, keys, colmajor=False, blk_keys=None):
            order = ([(kk, c0) for c0 in range(0, ncol, 1024) for kk in range(nk)] if colmajor else
                     [(kk, c0) for kk in range(nk) for c0 in range(0, ncol, 1024)])
            for kk, c0 in order:
                if True:
                    cw = min(1024, ncol - c0)
                    n = wl_n[0]
                    wl_n[0] += 1
                    s = n % len(stg)
                    k.dma('sp', stg[s][:, :cw], src2d[kk * 128:(kk + 1) * 128, c0:c0 + cw],
                          'wld%d' % s, w=[('stg', s)])
                    wk = (name, n)
                    keys.append(wk)
                    if blk_keys is not None:
                        blk_keys.setdefault(c0 // 1024, []).append(wk)

                    def conv(n=n, s=s, kk=kk, c0=c0, cw=cw, wk=wk):
                        e = ('dve', 'act')[n % 2]
                        o = dst[:, kk, c0:c0 + cw]
                        if gcol is None:
                            if e == 'act':
                                k.op(e, 'copy', o, stg[s][:, :cw], r=[('stg', s)], w=[wk])
                            else:
                                k.op(e, 'tensor_copy', o, stg[s][:, :cw], r=[('stg', s)], w=[wk])
                        else:
                            g = pv[:, gcol + kk:gcol + kk + 1]
                            if e == 'act':
                                k.op(e, 'activation', o, stg[s][:, :cw], AF.Copy, scale=g,
                                     r=[('stg', s), 'pv'], w=[wk])
                            else:
                                k.op(e, 'tensor_scalar', o, stg[s][:, :cw], g, None, ALU.mult,
                                     r=[('stg', s), 'pv'], w=[wk])
                    yield conv

        def load_weight(stg, dst, src2d, nk, ncol, gcol, name):
            keys = []
            for conv in load_weight_gen(stg, dst, src2d, nk, ncol, gcol, name, keys):
                conv()
            k.wait_keys('pe', keys)

        for lay in range(L):
            pb = lay * PVL
            x_src = xT_in if lay == 0 else xb
            last = (lay == L - 1)

            st12 = ExitStack()
            dw = sb("dw", [128, 31, 128], BF16, st12)
            with ExitStack() as st:
                winb = sb("winb", [128, 8, DIN], BF16, st)
                stg = [sb("p1_stg%d" % i, [128, 1024], F32, st) for i in range(4)]
                win_blk = {}
                win_gen = load_weight_gen(stg, winb, w_in[lay], 8, DIN, pb + 0, "win", [], colmajor=True,
                                          blk_keys=win_blk)
                win_waited = set()

                def need_w(col):
                    blk = col // 1024
                    if blk not in win_waited:
                        win_waited.add(blk)
                        k.wait_keys('pe', win_blk[blk])
                xt = [sb("p1_xt%d" % i, [128, 8, 512], F32, st) for i in range(2)]
                sq = sb("p1_sq", [128, 8, 512], BF16, st)
                hT = sb("p1_hT", [128, 8, 512], BF16, st)
                rs_t = sb("p1_rs_t", [128, 512], F32, st)
                rstd = sb("p1_rstd", [128, 512], F32, st)
                ug = sb("p1_ug", [128, 2, 512], F32, st)
                ya = sb("p1_ya", [128, 2, 512], F32, st)
                sqa = sb("p1_sqa", [128, 2, 512], BF16, st)
                tmpf = [sb("p1_tmpf%d" % i, [128, 512], F32, st) for i in range(2)]
                ob = [sb("p1_ob%d" % i, [128, 512], BF16, st) for i in range(12)]
                vg = sb("p1_vg", [128, 4, DG], F32, st)
                vjunk = sb("p1_vjunk", [128, DG], F32, st)
                ssv = sb("p1_ssv", [128, 4], F32, st)
                rv_t = sb("p1_rvt", [128, 4], F32, st)
                rv = sb("p1_rv", [128, 4], F32, st)
                vnp = [sb("p1_vnp%d" % i, [128, 4, 128], BF16, st) for i in range(4)]
                tmpf2 = [sb("p1_tmpf2_%d" % i, [128, 512], F32, st) for i in range(2)]
                wsT_f = sb("p1_wsTf", [128, 4, 128], F32, st)
                wsT = sb("p1_wsT", [128, 4, 128], BF16, st)
                bsT = sb("p1_bsT", [128, 2, 128], F32, st)
                vgb = sb("p1_vgb", [128, DG], F32, st)

                k.dma('sp', wsT_f[:], wsT_in[lay], 'ld_m0', w=['wsTf'])
                k.dma('sp', bsT[:], bsT_in[lay], 'ld_m1', w=['bsT'])
                k.dma('sp', vgb[:], vgb_in[lay], 'ld_m2', w=['vgb'])
                k.op('pool', 'affine_select', wsT[:], wsT_f[:], [[0, 4], [1, 128]], ALU.is_ge, 0.0,
                     base=0, channel_multiplier=-1, r=['wsTf'], w=['wsT'])
                for i in range(4):
                    k.op('pool', 'memset', vnp[i][:], 0.0, w=[('vnp', i)])

                SSB, VB0, FB0 = 0, 1, 3
                ZB = [5, 6, 7]
                zn = 0
                obn = 0
                x_v = x_src.ap().rearrange("(k p) n -> p k n", p=128)
                loc_v = loc.ap()
                tile_toks = []

                def exch_out(o, kind, i, t):
                    for hh in range(2):
                        if kind < 3:
                            dst = exchA[t].ap()[2 * i + hh, kind, :, :]
                        else:
                            dst = exchB[t].ap()[2 * i + hh, kind - 3, :, :]
                        out_dma(ob[o][64 * hh:64 * hh + 64, :], dst, ('ob', o))

                def zchunk(m, tsl):
                    nonlocal zn
                    need_w(m * 128)
                    b = ZB[zn % 3]
                    zn += 1
                    for kk in range(8):
                        k.op('pe', 'matmul', banks[b][:], winb[:, kk, m * 128:(m + 1) * 128], hT[:, kk, :],
                             start=(kk == 0), stop=(kk == 7), r=[('hT', kk)], w=[('bank', b)])
                    return b

                def out_dma(src_ap, dst_ap, rk):
                    nonlocal obn
                    tile_toks.append(k.dma('sp', dst_ap, src_ap, 'st_p1_%d' % (obn % 12),
                                           r=[rk]))

                def next_ob():
                    nonlocal obn
                    obn += 1
                    return obn % 12

                pend_cc = []
                cc1 = {}

                def flush_cc():
                    for tt, toks in pend_cc:
                        for tok in toks:
                            k.wait('pool', tok)
                        cc1[tt] = k.collective("AllGather", [exchB[tt].ap().rearrange("s k q n -> (s k q) n").opt()],
                                               [g1B[tt].ap().rearrange("r s k q n -> (r s k q) n").opt()])
                    pend_cc.clear()

                def prologue(t):
                    s = t % 2
                    k.op('act', 'activation', sq[:], xt[s][:], AF.Square, r=[('xt', s)], w=['sq'])
                    if t + 1 < NT:
                        k.dma('sp', xt[1 - s][:], x_v[:, :, (t + 1) * 512:(t + 2) * 512], 'ld_x%d' % (1 - s),
                              w=[('xt', 1 - s)])
                    for kk in range(8):
                        k.op('pe', 'matmul', banks[SSB][:], ones[:], sq[:, kk, :], start=(kk == 0), stop=(kk == 7),
                             r=['ones', 'sq'], w=[('bank', SSB)])
                    rstd_from(banks[SSB][:], D, rstd[:], rs_t[:], ('bank', SSB), 'rstd', 'rs_t')
                    for kk in range(8):
                        e = 'dve' if kk % 2 == 0 else 'pool'
                        k.op(e, 'tensor_tensor', hT[:, kk, :], xt[s][:, kk, :], rstd[:], ALU.mult,
                             r=[('xt', s), 'rstd'], w=[('hT', kk)])

                k.dma('sp', xt[0][:], x_v[:, :, 0:512], 'ld_x0', w=[('xt', 0)])
                prologue(0)
                for conv in win_gen:
                    conv()
                for t in range(NT):
                    s = t % 2
                    tsl = slice(t * 512, (t + 1) * 512)
                    if t > 0:
                        prologue(t)
                    flush_cc()
                    for s_ in range(31):
                        if s_ % NT == t:
                            k.op('pool', 'tensor_scalar', dw[:, s_, :], ident[:],
                                 pv[:, pb + 28 + s_:pb + 29 + s_], None, ALU.mult, r=['ident', 'pv'],
                                 w=[('dw', s_)])
                    need_w(DG)
                    for sub in range(4):
                        vbk = VB0 + sub // 2
                        for kk in range(8):
                            k.op('pe', 'matmul', banks[vbk][:, (sub % 2) * DG:(sub % 2 + 1) * DG],
                                 hT[:, kk, sub * 128:(sub + 1) * 128], winb[:, kk, DG:2 * DG],
                                 start=(kk == 0), stop=(kk == 7), r=[('hT', kk)], w=[('bank', vbk)])
                    ub = [zchunk(0, tsl), zchunk(1, tsl)]
                    for h2 in range(2):
                        k.op('act', 'activation', vg[:, 2 * h2:2 * h2 + 2, :].rearrange("p a d -> p (a d)"),
                             banks[VB0 + h2][:], AF.Gelu, r=[('bank', VB0 + h2)], w=[('vg', h2)])
                    for i in range(2):
                        k.op('act', 'activation', ug[:, i, :], banks[ub[i]][:], AF.Gelu, r=[('bank', ub[i])],
                             w=[('ug', i)])
                    for sub in range(4):
                        k.op('act', 'activation', vjunk[:], vg[:, sub, :], AF.Square, accum_out=ssv[:, sub:sub + 1],
                             r=[('vg', sub // 2)], w=['vjunk', ('ssv', sub)])
                    k.op('act', 'activation', rv_t[:], ssv[:], AF.Ln, bias=EPS, scale=1.0 / DG,
                         r=[('ssv', 0), ('ssv', 1), ('ssv', 2), ('ssv', 3)], w=['rv_t'])
                    k.op('act', 'activation', rv[:], rv_t[:], AF.Exp, scale=-0.5, r=['rv_t'], w=['rv'])
                    gbv = vgb[:].rearrange("p (c i d) -> p c i d", c=2, i=2)
                    for sub in range(4):
                        vgv = vg[:, sub, :].rearrange("p (c i d) -> p c i d", c=2, i=2)
                        vnv = vnp[sub][:].rearrange("p (c i) (j d) -> p c i j d", i=2, j=2)
                        for i in range(2):
                            k.op('dve', 'scalar_tensor_tensor', vnv[:, :, i, i, :], vgv[:, :, i, :], rv[:, sub:sub + 1],
                                 gbv[:, :, i, :], ALU.mult, ALU.mult,
                                 r=[('vg', sub // 2), 'rv', 'vgb'], w=[('vnp', sub)])
                    def c_chunk(kind, i):
                        b = zchunk(10 + 2 * kind + i, tsl)
                        o = next_ob()
                        if kind == 0:
                            k.op('dve', 'tensor_scalar', ob[o][:], banks[b][:], 0.125, None, ALU.mult,
                                 r=[('bank', b)], w=[('ob', o)])
                        elif kind == 1:
                            k.op('act', 'copy', ob[o][:], banks[b][:], r=[('bank', b)], w=[('ob', o)])
                        else:
                            k.op('dve', 'tensor_copy', ob[o][:], banks[b][:], r=[('bank', b)], w=[('ob', o)])
                        exch_out(o, kind, i, t)
                    for kind in range(3):
                        for i in range(2):
                            c_chunk(kind, i)
                    a_toks = list(tile_toks)
                    tile_toks.clear()
                    for i in range(2):
                        b = zchunk(4 + i, tsl)
                        o = next_ob()
                        k.op('act', 'copy', ob[o][:], banks[b][:], r=[('bank', b)], w=[('ob', o)])
                        out_dma(ob[o][:], loc_v[DG + i * 128:DG + (i + 1) * 128, tsl], ('ob', o))
                    for i in range(2):
                        b = zchunk(6 + i, tsl)
                        k.op('act', 'copy', tmpf[i][:], banks[b][:], r=[('bank', b)], w=[('tmpf', i)])
                        b = zchunk(8 + i, tsl)
                        o = next_ob()
                        k.op('dve', 'tensor_tensor', ob[o][:], banks[b][:], tmpf[i][:], ALU.mult,
                             r=[('bank', b), ('tmpf', i)], w=[('ob', o)])
                        exch_out(o, 3, i, t)
                    for tok in a_toks:
                        k.wait('pool', tok)
                    k.collective("AllGather", [exchA[t].ap().rearrange("s k q n -> (s k q) n").opt()],
                                 [g1A[t].ap().rearrange("r s k q n -> (r s k q) n").opt()])
                    for sub in range(4):
                        for c in range(2):
                            fb = FB0 + c
                            for i in range(2):
                                k.op('pe', 'matmul', banks[fb][:, sub * 128:(sub + 1) * 128],
                                     vnp[sub][:, 2 * c + i, :], wsT[:, 2 * c + i, :],
                                     start=(i == 0), stop=(i == 1), r=[('vnp', sub), 'wsT'], w=[('bank', fb)])
                    for c in range(2):
                        fb = FB0 + c
                        tf = tmpf2[c]
                        k.op('dve', 'tensor_tensor', tf[:].rearrange("p (a t) -> p a t", a=4),
                             banks[fb][:].rearrange("p (a t) -> p a t", a=4),
                             bsT[:, c, :].unsqueeze(1).to_broadcast([128, 4, 128]), ALU.add,
                             r=[('bank', fb), 'bsT'], w=[('tmpf2', c)])
                        k.op('pool', 'tensor_tensor', ya[:, c, :], tf[:], ug[:, c, :], ALU.mult,
                             r=[('tmpf2', c), ('ug', c)], w=[('ya', c)])
                    k.op('act', 'activation', sqa[:], ya[:], AF.Square, r=[('ya', 0), ('ya', 1)], w=['sqa'])
                    for c in range(2):
                        k.op('pe', 'matmul', banks[SSB][:], ones[:], sqa[:, c, :], start=(c == 0), stop=(c == 1),
                             r=['ones', 'sqa'], w=[('bank', SSB)])
                    rstd_from(banks[SSB][:], DG, rstd[:], rs_t[:], ('bank', SSB), 'rstd', 'rs_t')
                    for c in range(2):
                        o = next_ob()
                        k.op('dve' if c == 0 else 'pool', 'tensor_tensor', ob[o][:], ya[:, c, :], rstd[:], ALU.mult,
                             r=[('ya', c), 'rstd'], w=[('ob', o)])
                        out_dma(ob[o][:], loc_v[c * 128:(c + 1) * 128, tsl], ('ob', o))
                    for i in range(2):
                        b = zchunk(18 + i, tsl)
                        k.op('act', 'activation', tmpf[i][:], banks[b][:], AF.Sigmoid, r=[('bank', b)],
                             w=[('tmpf', i)])
                        b = zchunk(16 + i, tsl)
                        o = next_ob()
                        k.op('dve', 'tensor_tensor', ob[o][:], banks[b][:], tmpf[i][:], ALU.mult,
                             r=[('bank', b), ('tmpf', i)], w=[('ob', o)])
                        exch_out(o, 4, i, t)
                    pend_cc.append((t, list(tile_toks)))
                    tile_toks.clear()
                flush_cc()
                k.barrier(skip=('cc',))


            with ExitStack() as st:
                QKV = sb("QKV", [64, 3, S], BF16, st)
                CV = sb("CV", [128, 32 + S], BF16, st)
                Vtok = sb("Vtok", [128, NKB, 64], BF16, st)
                negtri = sb("negtri", [128, 128], BF16, st)
                mones = sb("mones", [128, 128], BF16, st)
                zer = sb("zer", [128, 512], BF16, st)
                mask = sb("mask", [128, 4, 512], BF16, st)
                Eb = [sb("Eb%d" % i, [128, 1024], F32, st) for i in range(2)]
                Lb = [sb("Lb%d" % i, [128, 1024], BF16, st) for i in range(3)]
                Wb = [sb("Wb%d" % i, [128, 1024], F32, st) for i in range(2)]
                Ab = [sb("Ab%d" % i, [128, 1024], BF16, st) for i in range(3)]
                r_sb = sb("r_sb", [128, 512], F32, st)
                osb = [sb("osb%d" % i, [64, 512], BF16, st) for i in range(2)]
                csb = [sb("csb%d" % i, [128, 512], BF16, st) for i in range(2)]

                setup = []
                for t in range(NT):
                    k.wait('sp', cc1[t])
                    for rr in range(4):
                        c0 = rr * TPC + t * 512
                        k.dma('sp', QKV[:, :, c0:c0 + 512],
                              g1A[t].ap()[rr, bass.ds(slot_sp, 1), :, :, :].rearrange("s k q n -> q (s k) n"),
                              'ld_p2_%d' % t, w=[('p2ldA', rr, t)])
                        k.dma('sp', CV[:, 32 + c0:32 + c0 + 512],
                              g1B[t].ap()[rr, bass.ds(slot_sp, 1), :, :, :].rearrange("s k q n -> (s k q) n"),
                              'ld_p2_%d' % t, w=[('p2ldB', rr, t)])
                setup += ['CVpad', 'negtri']
                k.op('pool', 'memset', CV[:, 0:32], 0.0, w=['CVpad'])
                k.op('pool', 'memset', mones[:], -1.0, w=['mones'])
                k.op('pool', 'memset', zer[:], 0.0, w=['zer'])
                k.op('pool', 'affine_select', negtri[:], mones[:], [[-1, 128]], ALU.is_ge, 0.0,
                     base=0, channel_multiplier=1, r=['mones'], w=['negtri'])
                for o in range(4):
                    k.op('pool', 'affine_select', mask[:, o, :], zer[:], [[1, 512]], ALU.is_gt, NEG,
                         base=-128 * o, channel_multiplier=-1, r=['zer'], w=[('mask', o)])
                    setup.append(('mask', o))

                CB = 4
                OB = [5, 6]
                XB = 7
                vtp = banks[XB][:].bitcast(BF16)
                k.wait_keys('pe', setup)

                def prep_tile(qt):
                    rr_, t_ = divmod(qt, NT)
                    k.wait_keys('pe', [('p2ldA', rr_, t_), ('p2ldB', rr_, t_)])
                    tb = OB[qt % 2]
                    vt_ = banks[tb][:].bitcast(BF16)
                    for bi in range(4):
                        k.op('pe', 'transpose', vt_[:, bi * 64:(bi + 1) * 64],
                             QKV[:, 2, (4 * qt + bi) * 128:(4 * qt + bi + 1) * 128], ident[0:64, 0:64],
                             r=['ident'], w=[('bank', tb)])
                    k.op('dve', 'tensor_copy', Vtok[:, 4 * qt:4 * qt + 4, :].rearrange("p a d -> p (a d)"),
                         vt_[:, 0:256], r=[('bank', tb)], w=[('Vtok', qt)])
                x2_toks = []
                cc2 = {}
                pairs = []
                for qt in range(NQT):
                    npair = 2 * qt + 2
                    for p in range(npair):
                        kb0 = 4 * qt + 3 - 2 * p
                        pairs.append((qt, kb0, p, npair))
                NP = len(pairs)

                def zbank(g, j):
                    return pp[g % 2][:, j * 512:(j + 1) * 512]

                prepped = set()

                def s1(g):
                    qt, kb0, p, npair = pairs[g]
                    if p == 0 and qt not in prepped:
                        prepped.add(qt)
                        prep_tile(qt)
                    if p == 3 and NT <= qt + 1 < NQT:
                        prepped.add(qt + 1)
                        prep_tile(qt + 1)
                    for j in range(2):
                        kb = kb0 - j
                        o = kb - 4 * qt
                        k.op('pe', 'matmul', zbank(g, j), QKV[:, 1, kb * 128:(kb + 1) * 128],
                             QKV[:, 0, qt * 512:(qt + 1) * 512], start=True, stop=(o < 0), w=[('zp', g % 2)])
                        if o >= 0:
                            k.op('pe', 'matmul', zbank(g, j), ident[:], mask[:, o, :], start=False, stop=True,
                                 w=[('zp', g % 2)])

                def s2(g):
                    k.op('act', 'activation', Eb[g % 2][:], pp[g % 2][:], AF.Exp, r=[('zp', g % 2)], w=[('E', g % 2)],
                         same_ok=True)
                    k.op('act', 'activation', Lb[g % 3][:], Eb[g % 2][:], AF.Ln, bias=1.0, scale=1.0,
                         r=[('E', g % 2)], w=[('L', g % 3)], same_ok=True)

                def s3(g):
                    qt, kb0, p, npair = pairs[g]
                    L0 = Lb[g % 3][:, 0:512]
                    L1 = Lb[g % 3][:, 512:1024]
                    k.op('pe', 'matmul', zbank(g, 0), negtri[:], L0, start=False, stop=True,
                         r=[('L', g % 3)], w=[('zp', g % 2)])
                    k.op('pe', 'matmul', zbank(g, 1), negtri[:], L1, start=False, stop=False,
                         r=[('L', g % 3)], w=[('zp', g % 2)])
                    k.op('pe', 'matmul', zbank(g, 1), mones[:], L0, start=False, stop=True,
                         r=[('L', g % 3)], w=[('zp', g % 2)])
                    if p < npair - 1:
                        k.op('pe', 'matmul', banks[CB][:], ones[:], L0, start=True, stop=False,
                             r=[('L', g % 3)], w=[('bank', CB)])
                        k.op('pe', 'matmul', banks[CB][:], ones[:], L1, start=False, stop=True,
                             r=[('L', g % 3)], w=[('bank', CB)])

                def s4(g):
                    qt, kb0, p, npair = pairs[g]
                    if p == 0:
                        k.op('dve', 'tensor_copy', Wb[g % 2][:], pp[g % 2][:], r=[('zp', g % 2)], w=[('W', g % 2)])
                        if npair > 1:
                            k.op('dve', 'tensor_copy', r_sb[:], banks[CB][:], r=[('bank', CB)], w=['r'])
                    else:
                        k.op('dve', 'tensor_tensor', Wb[g % 2][:].rearrange("p (a n) -> p a n", a=2),
                             pp[g % 2][:].rearrange("p (a n) -> p a n", a=2),
                             r_sb[:].unsqueeze(1).to_broadcast([128, 2, 512]), ALU.subtract,
                             r=[('zp', g % 2), 'r'], w=[('W', g % 2)], same_ok=True)
                        if p < npair - 1:
                            k.op('dve', 'tensor_tensor', r_sb[:], banks[CB][:], r_sb[:], ALU.add,
                                 r=[('bank', CB), 'r'], w=['r'], same_ok=True)

                def s5(g):
                    k.op('act', 'activation', Ab[g % 3][:], Wb[g % 2][:], AF.Exp, r=[('W', g % 2)], w=[('A', g % 3)])

                def s6(g):
                    qt, kb0, p, npair = pairs[g]
                    obk = OB[qt % 2]
                    for j in range(2):
                        k.op('pe', 'matmul', banks[obk][0:64, :], Vtok[:, kb0 - j, :], Ab[g % 3][:, j * 512:(j + 1) * 512],
                             start=(p == 0 and j == 0), stop=(p == npair - 1 and j == 1),
                             r=[('A', g % 3), ('Vtok', (kb0 - j) // 4)], w=[('bank', obk)])
                    tpp = -(-31 // npair)
                    for s_ in range(p * tpp, min(31, (p + 1) * tpp)):
                        c0 = 32 + qt * 512 - s_
                        k.op('pe', 'matmul', banks[XB][:], dw[:, s_, :], CV[:, c0:c0 + 512],
                             start=(s_ == 0), stop=(s_ == 30), w=[('bank', XB)])
                    if p == npair - 1:
                        s = qt % 2
                        k.op('dve', 'tensor_copy', osb[s][:], banks[obk][0:64, :], r=[('bank', obk)], w=[('osb', s)])
                        pc, po = qt % NT, qt // NT
                        x2_toks.append(k.dma('sp', x2b[pc].ap()[0, :, po, :], osb[s][:], 'st_o%d' % s,
                                             r=[('osb', s)]))
                        k.op('dve', 'tensor_copy', csb[s][:], banks[XB][:], r=[('bank', XB)], w=[('csb', s)])
                        x2_toks.append(k.dma('sp', x2b[pc].ap()[1:3, :, po, :].rearrange("p q n -> (p q) n"),
                                             csb[s][:], 'st_c%d' % s, r=[('csb', s)]))
                        if po == 3:
                            for tok in x2_toks:
                                k.wait('pool', tok)
                            cc2[pc] = k.collective(
                                "AllGather", [x2b[pc].ap().rearrange("p q s n -> (p q) (s n)").opt()],
                                [g2[pc].ap().rearrange("r p q s n -> (r p q) (s n)").opt()])

                for step in range(NP + 3):
                    if step < NP:
                        s1(step)
                        s2(step)
                    if 0 <= step - 1 < NP:
                        s3(step - 1)
                        s4(step - 1)
                    if 0 <= step - 2 < NP:
                        s5(step - 2)
                    if 0 <= step - 3 < NP:
                        s6(step - 3)
                k.barrier(skip=('cc',))
            if dbg == 'p2' and lay == 0:
                for t in range(NT):
                    k.dma('sp', dbg_out[t][:, :, :, :], x2b[t][:, :, :, :], 'dbg')
                k.barrier()
                return nc


            st12.close()
            st3 = ExitStack()
            wupb = sb("wupb", [128, 8, DFF], BF16, st3)
            wup_keys = []
            with ExitStack() as st:
                stg = [sb("p3_stg%d" % i, [128, 1024], F32, st) for i in range(4)]
                woutb = sb("woutb", [128, 8, D], BF16, st)
                wout_keys = []
                for conv in load_weight_gen(stg, woutb, w_out[lay], 8, D, pb + 16, "wout", wout_keys):
                    conv()
                wup_gen = load_weight_gen(stg, wupb, w_up[lay], 8, DFF, pb + 8, "wup", wup_keys)
                xt = [sb("p3_xt%d" % i, [128, 8, 512], F32, st) for i in range(2)]
                ycats = [sb("p3_ycat%d" % i, [128, 8, 512], BF16, st) for i in range(2)]
                gin = [sb("p3_gin%d" % i, [128, 6, 512], BF16, st) for i in range(2)]
                gbt = [sb("p3_gb%d" % i, [128, 2, 512], BF16, st) for i in range(2)]
                yb = sb("p3_yb", [128, 2, 512], F32, st)
                xc = sb("p3_xc", [128, 2, 512], F32, st)
                yd = sb("p3_yd", [128, 2, 512], F32, st)
                sq3 = sb("p3_sq3", [128, 3, 1024], BF16, st)
                rs_t3 = sb("p3_rs_t3", [128, 3, 512], F32, st)
                rstd3 = sb("p3_rstd3", [128, 3, 512], F32, st)
                x_v = x_src.ap().rearrange("(k p) n -> p k n", p=128)
                xa_v = xa.ap().rearrange("(k p) n -> p k n", p=128)
                loc_v = loc.ap()
                MBK = 0
                SBK = [1, 2, 3]
                WB = [4, 5, 6]
                wn = 0

                def p3_loads(t):
                    s = t % 2
                    tsl = slice(t * 512, (t + 1) * 512)
                    k.dma('sp', xt[s][:], x_v[:, :, tsl], 'ld_x%d' % s, w=[('xt', s)])
                    k.dma('sp', ycats[s][:, 0:2, :], loc_v[0:DG, tsl].rearrange("(c p) n -> p c n", p=128),
                          'ld_ya%d' % s, w=[('ycat', s, 0), ('ycat', s, 1)])
                    k.dma('sp', gbt[s][:], loc_v[DG:2 * DG, tsl].rearrange("(c p) n -> p c n", p=128),
                          'ld_gb%d' % s, w=[('gb', s)])
                    k.wait('sp', cc2[t])
                    for rr in range(4):
                        k.dma('sp', gin[s][64 * (rr % 2):64 * (rr % 2) + 64, (rr // 2)::2, :],
                              g2[t].ap()[rr, :, :, bass.ds(slot_sp, 1), :].rearrange("p q s n -> q (p s) n"),
                              'ld_g%d' % s, w=[('gin', s, rr)])

                def mul2(dst_fn, src_fn, rs_ap, rkeys, wkeys):
                    for c in range(2):
                        k.op('dve' if c == 0 else 'pool', 'tensor_tensor', dst_fn(c), src_fn(c), rs_ap, ALU.mult,
                             r=rkeys(c), w=wkeys(c))

                p3_loads(0)
                for t in range(NT):
                    s = t % 2
                    tsl = slice(t * 512, (t + 1) * 512)
                    ycat = ycats[s]
                    GIN = [('gin', s, rr) for rr in range(4)]
                    if t + 1 < NT:
                        p3_loads(t + 1)
                    quota = -(-32 // NT)
                    convs = [next(wup_gen, None) for _ in range(min(quota, len(stg)))]
                    for c in range(2):
                        k.op('pool', 'tensor_tensor', yb[:, c, :], gbt[s][:, c, :], gin[s][:, 2 + c, :], ALU.mult,
                             r=[('gb', s)] + GIN, w=[('yb', c)])
                    for c in range(2):
                        k.op('pe', 'matmul', banks[MBK][:], ones[:], gin[s][:, 4 + c, :], start=(c == 0), stop=(c == 1),
                             r=['ones'] + GIN, w=[('bank', MBK)])
                    for c in range(2):
                        k.op('dve', 'scalar_tensor_tensor', xc[:, c, :], banks[MBK][:], -1.0 / DG, gin[s][:, 4 + c, :],
                             ALU.mult, ALU.add, r=[('bank', MBK)] + GIN, w=[('xc', c)])
                    k.op('act', 'activation', sq3[:, 0, :], yb[:].rearrange("p c n -> p (c n)"), AF.Square,
                         r=[('yb', 0), ('yb', 1)], w=[('sq3', 0)])
                    k.op('act', 'activation', sq3[:, 1, :], gin[s][:, 0:2, :].rearrange("p c n -> p (c n)"), AF.Square,
                         r=GIN, w=[('sq3', 1)])
                    k.op('act', 'activation', sq3[:, 2, :], xc[:].rearrange("p c n -> p (c n)"), AF.Square,
                         r=[('xc', 0), ('xc', 1)], w=[('sq3', 2)])
                    for g_ in range(3):
                        for c in range(2):
                            k.op('pe', 'matmul', banks[SBK[g_]][:], ones[:], sq3[:, g_, c * 512:(c + 1) * 512],
                                 start=(c == 0), stop=(c == 1), r=['ones', ('sq3', g_)], w=[('bank', SBK[g_])])
                    for g_ in range(3):
                        k.op('act', 'activation', rs_t3[:, g_, :], banks[SBK[g_]][:], AF.Ln, bias=EPS,
                             scale=1.0 / DG, r=[('bank', SBK[g_])], w=[('rs_t3', g_)])
                    k.op('act', 'activation', rstd3[:].rearrange("p g n -> p (g n)"),
                         rs_t3[:].rearrange("p g n -> p (g n)"), AF.Exp, scale=-0.5,
                         r=[('rs_t3', 0), ('rs_t3', 1), ('rs_t3', 2)], w=['rstd3'])
                    mul2(lambda c: xc[:, c, :], lambda c: xc[:, c, :], rstd3[:, 2, :],
                         lambda c: [('xc', c), 'rstd3'], lambda c: [('xc', c)])
                    mul2(lambda c: ycat[:, 2 + c, :], lambda c: yb[:, c, :], rstd3[:, 0, :],
                         lambda c: [('yb', c), 'rstd3'], lambda c: [('ycat', s, 2 + c)])
                    mul2(lambda c: ycat[:, 4 + c, :], lambda c: gin[s][:, c, :], rstd3[:, 1, :],
                         lambda c: GIN + ['rstd3'], lambda c: [('ycat', s, 4 + c)])
                    for c in range(2):
                        k.op('act', 'activation', yd[:, c, :], xc[:, c, :], AF.Silu,
                             bias=pv[:, pb + 26 + c:pb + 27 + c], scale=pv[:, pb + 24 + c:pb + 25 + c],
                             r=[('xc', c), 'pv'], w=[('yd', c)])
                    k.op('act', 'activation', sq3[:, 0, :], yd[:].rearrange("p c n -> p (c n)"), AF.Square,
                         r=[('yd', 0), ('yd', 1)], w=[('sq3', 0)])
                    for c in range(2):
                        k.op('pe', 'matmul', banks[SBK[0]][:], ones[:], sq3[:, 0, c * 512:(c + 1) * 512],
                             start=(c == 0), stop=(c == 1), r=['ones', ('sq3', 0)], w=[('bank', SBK[0])])
                    k.op('act', 'activation', rs_t3[:, 0, :], banks[SBK[0]][:], AF.Ln, bias=EPS,
                         scale=1.0 / DG, r=[('bank', SBK[0])], w=[('rs_t3', 0)])
                    k.op('act', 'activation', rstd3[:, 0, :], rs_t3[:, 0, :], AF.Exp, scale=-0.5,
                         r=[('rs_t3', 0)], w=['rstd3'])
                    mul2(lambda c: ycat[:, 6 + c, :], lambda c: yd[:, c, :], rstd3[:, 0, :],
                         lambda c: [('yd', c), 'rstd3'], lambda c: [('ycat', s, 6 + c)])
                    if t == 0:
                        k.wait_keys('pe', wout_keys)
                    for m in range(8):
                        b = WB[wn % 3]
                        wn += 1
                        for kk in range(8):
                            k.op('pe', 'matmul', banks[b][:], woutb[:, kk, m * 128:(m + 1) * 128], ycat[:, kk, :],
                                 start=(kk == 0), stop=(kk == 7), r=[('ycat', s, kk)], w=[('bank', b)])
                        k.op('dve', 'tensor_tensor', xt[s][:, m, :], banks[b][:], xt[s][:, m, :], ALU.add,
                             r=[('bank', b), ('xt', s)], w=[('xt', s)])
                    k.dma('sp', xa_v[:, :, tsl], xt[s][:], 'st_x%d' % s, r=[('xt', s)])
                    for conv in convs:
                        if conv is not None:
                            conv()
                    for _ in range(quota - len(convs)):
                        conv = next(wup_gen, None)
                        if conv is not None:
                            conv()
                for conv in wup_gen:
                    conv()
                k.barrier()
            if dbg == 'p3a' and lay == 0:
                k.dma('sp', dbg_out[0][:, :], xa[:, :], 'dbg')
                k.barrier()
                return nc

            with ExitStack() as st:
                TF = 512
                NF = TPC // TF
                wdnb = sb("wdnb", [128, 32, D], BF16, st)
                k.wait_keys('pe', wup_keys)
                xt0 = sb("f_xt0", [128, 8, TF], F32, st)
                hT = sb("f_hT", [128, 8, TF], BF16, st)
                aT = sb("f_aT", [128, 32, TF], BF16, st)
                sq = aT[:, 0:8, :]
                SQW = ['sq'] + [('aT', m) for m in range(8)]
                rl = [sb("f_rl%d" % i, [128, TF], BF16, st) for i in range(2)]
                rstd = sb("f_rstd", [128, TF], F32, st)
                stB = ExitStack()
                stgB = [sb("f_stg%d" % i, [128, 1024], F32, stB) for i in range(4)]
                wdn_keys = []
                wdn_gen = load_weight_gen(stgB, wdnb, w_down[lay], 32, D, None, "wdn", wdn_keys)
                xt = [xt0, None]
                xa_v = xa.ap().rearrange("(k p) n -> p k n", p=128)
                dst = outT if last else xb
                dst_v = dst.ap().rearrange("(k p) n -> p k n", p=128)
                un = 0
                k.dma('sp', xt[0][:], xa_v[:, :, 0:TF], 'ld_x0', w=[('xt', 0)])
                for t in range(NF):
                    s = t % 2
                    tsl = slice(t * TF, (t + 1) * TF)
                    if t >= 1 and t + 1 < NF:
                        k.dma('sp', xt[1 - s][:], xa_v[:, :, (t + 1) * TF:(t + 2) * TF], 'ld_x%d' % (1 - s),
                              w=[('xt', 1 - s)])
                    k.op('act', 'activation', sq, xt[s][:], AF.Square, r=[('xt', s)], w=SQW)
                    for kk in range(8):
                        k.op('pe', 'matmul', banks[0][:, 0:TF], ones[:], sq[:, kk, :], start=(kk == 0), stop=(kk == 7),
                             r=['ones', 'sq'], w=[('bank', 0)])
                    rstd_from(banks[0][:, 0:TF], D, rstd[:], rstd[:], ('bank', 0), 'rstd', 'rstd')
                    for kk in range(8):
                        k.op('dve' if kk % 2 == 0 else 'pool', 'tensor_tensor', hT[:, kk, :], xt[s][:, kk, :], rstd[:],
                             ALU.mult, r=[('xt', s), 'rstd'], w=[('hT', kk)])
                    for m in range(32):
                        hb = un % 6
                        un += 1
                        pso = banks[1 + hb][:, 0:TF]
                        for kk in range(8):
                            k.op('pe', 'matmul', pso, wupb[:, kk, m * 128:(m + 1) * 128], hT[:, kk, :],
                                 start=(kk == 0), stop=(kk == 7), r=[('hT', kk)], w=[('bank', 1 + hb)])
                        ri = m % 2
                        if m % 2 == 0:
                            k.op('act', 'activation', rl[ri][:], pso, AF.Relu, r=[('bank', 1 + hb)], w=[('rl', ri)])
                        else:
                            k.op('dve', 'tensor_scalar', rl[ri][:], pso, 0.0, None, ALU.max,
                                 r=[('bank', 1 + hb)], w=[('rl', ri)])
                        k.op('pool', 'tensor_tensor', aT[:, m, :], rl[ri][:], rl[ri][:], ALU.mult,
                             r=[('rl', ri)], w=([('aT', m), 'sq'] if m < 8 else [('aT', m)]))
                        if t == 0:
                            conv = next(wdn_gen, None)
                            if conv is not None:
                                conv()
                    if t == 0:
                        for conv in wdn_gen:
                            conv()
                        k.wait_keys('pe', wdn_keys)
                        if NF > 1:
                            k.wait_keys('sp', wdn_keys)
                            stB.close()
                            xt[1] = sb("f_xt1", [128, 8, TF], F32, st)
                            k.dma('sp', xt[1][:], xa_v[:, :, TF:2 * TF], 'ld_x1', w=[('xt', 1)])
                    for m in range(8):
                        hb = un % 6
                        un += 1
                        pso = banks[1 + hb][:, 0:TF]
                        for kk in range(32):
                            k.op('pe', 'matmul', pso, wdnb[:, kk, m * 128:(m + 1) * 128], aT[:, kk, :],
                                 start=(kk == 0), stop=(kk == 31), r=[('aT', kk)], w=[('bank', 1 + hb)])
                        k.op('dve', 'tensor_tensor', xt[s][:, m, :], pso, xt[s][:, m, :], ALU.add,
                             r=[('bank', 1 + hb), ('xt', s)], w=[('xt', s)])
                    if last:
                        k.op('act', 'activation', sq, xt[s][:], AF.Square, r=[('xt', s)], w=SQW)
                        for kk in range(8):
                            k.op('pe', 'matmul', banks[7][:, 0:TF], ones[:], sq[:, kk, :], start=(kk == 0),
                                 stop=(kk == 7), r=['ones', 'sq'], w=[('bank', 7)])
                        rstd_from(banks[7][:, 0:TF], D, rstd[:], rstd[:], ('bank', 7), 'rstd', 'rstd')
                        for kk in range(8):
                            k.op('dve', 'scalar_tensor_tensor', xt[s][:, kk, :], xt[s][:, kk, :],
                                 pv[:, PVL * L + kk:PVL * L + kk + 1], rstd[:], ALU.mult, ALU.mult,
                                 r=[('xt', s), 'pv', 'rstd'], w=[('xt', s)])
                    k.dma('sp', dst_v[:, :, tsl], xt[s][:], 'st_x%d' % s, r=[('xt', s)])
                k.barrier()
                if NF == 1:
                    stB.close()
            st3.close()
    return nc


def host_inputs(S, L, x, norm_mix_g, w_in, gmlp_v_g, gmlp_w_s, gmlp_b_s, short_conv_w, conf_conv_w,
                conf_ln_g, conf_ln_b, mix_out_g, w_out, norm_ffn_g, w_up, w_down, final_norm_g):
    TPC = S // 4
    f = lambda a: np.ascontiguousarray(np.asarray(a, dtype=np.float32))
    x = f(x)
    cols = lambda v: f(v).reshape(-1, 128).T
    vgb = np.ascontiguousarray(np.broadcast_to(f(gmlp_v_g)[:, None, :], (L, 128, DG)))
    wsT = np.ascontiguousarray(f(gmlp_w_s).transpose(0, 3, 1, 2))
    bs = f(gmlp_b_s)
    bsT = np.ascontiguousarray(np.repeat(bs.reshape(L, 2, 2, 1, 128), 64, axis=3).reshape(L, 2, 128, 128)
                               .transpose(0, 2, 1, 3))
    scw = f(short_conv_w)
    ccw = f(conf_conv_w)
    maps = []
    for c in range(8):
        b, j = c // 4, c % 4
        pvs = []
        for l in range(L):
            cw = np.zeros((128, 31), np.float32)
            for s_ in range(31):
                cw[64:, s_] = ccw[l, 30 - s_, 64 * j:64 * j + 64]
                if s_ <= 2:
                    cw[:64, s_] = scw[l, 2 - s_, 64 * j:64 * j + 64]
            pvs += [cols(norm_mix_g[l]), cols(norm_ffn_g[l]), cols(mix_out_g[l]), cols(conf_ln_g[l]),
                    cols(conf_ln_b[l]), cw]
        pvs.append(cols(final_norm_g))
        pv = np.ascontiguousarray(np.concatenate(pvs, axis=1))
        maps.append({
            "xT": np.ascontiguousarray(x[b, j * TPC:(j + 1) * TPC, :].T),
            "w_in": f(w_in), "w_out": f(w_out), "w_up": f(w_up), "w_down": f(w_down),
            "pv": pv, "vgb": vgb, "wsT": wsT, "bsT": bsT,
        })
    return maps


_NC_CACHE = {}


def run(S, L, inputs, dbg=None):
    key = (S, L, dbg)
    if key not in _NC_CACHE:
        _NC_CACHE[key] = build(S, L, dbg)
    nc = _NC_CACHE[key]
    maps = host_inputs(S, L, **inputs)
    res = run_bass_kernel_spmd(nc, maps, core_ids=list(range(8)))
    return res.results


def kernel(**inputs):
    x = inputs["x"]
    B, S, _ = x.shape
    L = inputs["w_in"].shape[0]
    TPC = S // 4
    results = run(S, L, inputs)
    out = np.empty((B, S, D), np.float32)
    for c in range(8):
        b, j = c // 4, c % 4
        out[b, j * TPC:(j + 1) * TPC, :] = results[c]["outT"].T
    return out
```

```python
from contextlib import ExitStack
import numpy as np
import concourse.bass as bass
import concourse.mybir as mybir
from concourse.bass_utils import run_bass_kernel_spmd

F32 = mybir.dt.float32
BF16 = mybir.dt.bfloat16
AF = mybir.ActivationFunctionType
ALU = mybir.AluOpType
ET = mybir.EngineType

D = 1024
DG = 256
DFF = 4096
DIN = 2560
EPS = 1e-6
NEG = -240.0
PVL = 59
GROUPS = [[0, 1, 2, 3], [4, 5, 6, 7]]


class K:
    def __init__(self, nc):
        self.nc = nc
        self.eng = {'pe': nc.tensor, 'act': nc.scalar, 'dve': nc.vector, 'pool': nc.gpsimd, 'sp': nc.sync}
        self.sem = {}
        self.cnt = {}
        self.seen = {}
        self.lastw = {}
        self.readers = {}
        for e in ('pe', 'act', 'dve', 'pool'):
            self.sem[e] = nc.alloc_semaphore("s_" + e)
            self.cnt[e] = 0

    def _key(self, key):
        if key not in self.sem:
            self.sem[key] = self.nc.alloc_semaphore("d_" + str(key))
            self.cnt[key] = 0
        return self.sem[key]

    def wait(self, consumer, tok, skip_same=False):
        key, val = tok
        if consumer == 'pe' and key == 'pe':
            return
        if skip_same and key == consumer:
            return
        if key not in ('pe', 'act', 'dve', 'pool', 'cc'):
            val = self.cnt[key]
        if self.seen.get((consumer, key), 0) >= val:
            return
        self.seen[(consumer, key)] = val
        self.eng[consumer].wait_ge(self.sem[key], val)

    def _deps(self, e, reads, writes, skip_same=False):
        for b in reads:
            if b in self.lastw:
                self.wait(e, self.lastw[b], skip_same)
        for b in writes:
            if b in self.lastw:
                self.wait(e, self.lastw[b], skip_same)
            for t in self.readers.get(b, {}).items():
                self.wait(e, t, skip_same)

    def _commit(self, tok, reads, writes):
        key, val = tok
        for b in reads:
            r = self.readers.setdefault(b, {})
            if r.get(key, 0) < val:
                r[key] = val
        for b in writes:
            self.lastw[b] = tok
            self.readers[b] = {}

    def op(self, e, name, *args, r=(), w=(), same_ok=False, **kw):
        self._deps(e, r, w, skip_same=same_ok)
        ins = getattr(self.eng[e], name)(*args, **kw)
        self.cnt[e] += 1
        ins.then_inc(self.sem[e], 1)
        tok = (e, self.cnt[e])
        self._commit(tok, r, w)
        return tok

    def dma(self, q, out, in_, key, r=(), w=()):
        self._deps(q, r, w)
        sem = self._key(key)
        self.cnt[key] += 16
        self.eng[q].dma_start(out=out, in_=in_).then_inc(sem, 16)
        tok = (key, self.cnt[key])
        self._commit(tok, r, w)
        return tok

    def collective(self, kind, ins, outs, r=(), w=()):
        self._deps('pool', r, w)
        sem = self._key('cc')
        self.cnt['cc'] += 1
        self.nc.gpsimd.collective_compute(kind, ALU.bypass, replica_groups=GROUPS,
                                          ins=ins, outs=outs).then_inc(sem, 1)
        tok = ('cc', self.cnt['cc'])
        self._commit(tok, r, w)
        return tok

    def wait_keys(self, e, keys):
        for b in keys:
            if b in self.lastw:
                self.wait(e, self.lastw[b])

    def barrier(self, skip=()):
        for e in ('pe', 'act', 'dve', 'pool', 'sp'):
            for key, c in self.cnt.items():
                if c > 0 and key not in skip:
                    self.wait(e, (key, c))
        self.lastw = {}
        self.readers = {}


def build(S, L, dbg=None):
    TPC = S // 4
    NT = TPC // 512
    NQT = S // 512
    NKB = S // 128
    nc = bass.Bass("TRN2", target_bir_lowering=False)
    k = K(nc)
    NPV = PVL * L + 8

    xT_in = nc.dram_tensor("xT", [D, TPC], F32, kind="ExternalInput")
    w_in = nc.dram_tensor("w_in", [L, D, DIN], F32, kind="ExternalInput")
    w_out = nc.dram_tensor("w_out", [L, D, D], F32, kind="ExternalInput")
    w_up = nc.dram_tensor("w_up", [L, D, DFF], F32, kind="ExternalInput")
    w_down = nc.dram_tensor("w_down", [L, DFF, D], F32, kind="ExternalInput")
    pv_in = nc.dram_tensor("pv", [128, NPV], F32, kind="ExternalInput")
    vgb_in = nc.dram_tensor("vgb", [L, 128, DG], F32, kind="ExternalInput")
    wsT_in = nc.dram_tensor("wsT", [L, 128, 4, 128], F32, kind="ExternalInput")
    bsT_in = nc.dram_tensor("bsT", [L, 128, 2, 128], F32, kind="ExternalInput")
    outT = nc.dram_tensor("outT", [D, TPC], F32, kind="ExternalOutput")

    exchA = [nc.dram_tensor("exchA%d" % t, [4, 3, 64, 512], BF16) for t in range(NT)]
    exchB = [nc.dram_tensor("exchB%d" % t, [4, 2, 64, 512], BF16) for t in range(NT)]
    g1A = [nc.dram_tensor("g1A%d" % t, [4, 4, 3, 64, 512], BF16) for t in range(NT)]
    g1B = [nc.dram_tensor("g1B%d" % t, [4, 4, 2, 64, 512], BF16) for t in range(NT)]
    x2b = [nc.dram_tensor("x2b%d" % t, [3, 64, 4, 512], BF16) for t in range(NT)]
    g2 = [nc.dram_tensor("g2_%d" % t, [4, 3, 64, 4, 512], BF16) for t in range(NT)]
    loc = nc.dram_tensor("loc", [2 * DG, TPC], BF16)
    xa = nc.dram_tensor("xa", [D, TPC], F32)
    xb = nc.dram_tensor("xb", [D, TPC], F32)
    dbg_out = None
    if dbg == 'p2':
        dbg_out = tuple(nc.dram_tensor("dbg_x2b%d" % t, [3, 64, 4, 512], BF16, kind="ExternalOutput")
                        for t in range(NT))
    if dbg in ('p3a_ld', 'p3a_n', 'p3a_d', 'p3a_o'):
        dbg_out = (nc.dram_tensor("dbg_gin", [128, 6, 512], BF16, kind="ExternalOutput"),
                   nc.dram_tensor("dbg_ycat", [128, 8, 512], BF16, kind="ExternalOutput"))
    if dbg == 'p3a':
        dbg_out = (nc.dram_tensor("dbg_xa", [D, TPC], F32, kind="ExternalOutput"),)


    slot_sp = nc.partition_id([ET.SP]) % 4

    with ExitStack() as gs:
        uid = [0]

        def sb(name, shape, dt, st=gs):
            uid[0] += 1
            return st.enter_context(nc.sbuf_tensor("%s_%d" % (name, uid[0]), shape, dt))

        def ps(name, shape, dt, st=gs):
            return st.enter_context(nc.psum_tensor(name, shape, dt))

        pv = sb("pv_sb", [128, NPV], F32)
        ones = sb("ones", [128, 128], BF16)
        ident = sb("ident", [128, 128], BF16)
        epsc = sb("epsc", [128, 1], F32)
        onec = sb("onec", [128, 1], F32)
        k.dma('sp', pv[:], pv_in[:, :], 'ld_pv', w=['pv'])
        k.op('pool', 'memset', ones[:], 1.0, w=['ones'])
        k.op('pool', 'memset', epsc[:], EPS, w=['epsc'])
        k.op('pool', 'memset', onec[:], 1.0, w=['onec'])
        k.op('pool', 'affine_select', ident[:], ones[:], [[-1, 128]], ALU.is_equal, 0.0,
             base=0, channel_multiplier=1, r=['ones'], w=['ident'])
        pp = [ps("pp%d" % i, [128, 1024], F32) for i in range(4)]
        banks = [pp[i // 2][:, (i % 2) * 512:(i % 2 + 1) * 512] for i in range(8)]

        def rstd_from(psum_ap, n, out_ap, tmp_ap, rk, wk, tk):
            k.op('act', 'activation', tmp_ap, psum_ap, AF.Ln, bias=EPS, scale=1.0 / n, r=[rk], w=[tk])
            k.op('act', 'activation', out_ap, tmp_ap, AF.Exp, scale=-0.5, r=[tk], w=[wk])

        wl_n = [0]

        def load_weight_gen(stg, dst, src2d, nk, ncol, gcol, name, keys, colmajor=False, blk_keys=None):
            order = ([(kk, c0) for c0 in range(0, ncol, 1024) for kk in range(nk)] if colmajor else
                     [(kk, c0) for kk in range(nk) for c0 in range(0, ncol, 1024)])
            for kk, c0 in order:
                if True:
                    cw = min(1024, ncol - c0)
                    n = wl_n[0]
                    wl_n[0] += 1
                    s = n % len(stg)
                    k.dma('sp', stg[s][:, :cw], src2d[kk * 128:(kk + 1) * 128, c0:c0 + cw],
                          'wld%d' % s, w=[('stg', s)])
                    wk = (name, n)
                    keys.append(wk)
                    if blk_keys is not None:
                        blk_keys.setdefault(c0 // 1024, []).append(wk)

                    def conv(n=n, s=s, kk=kk, c0=c0, cw=cw, wk=wk):
                        e = ('dve', 'act')[n % 2]
                        o = dst[:, kk, c0:c0 + cw]
                        if gcol is None:
                            if e == 'act':
                                k.op(e, 'copy', o, stg[s][:, :cw], r=[('stg', s)], w=[wk])
                            else:
                                k.op(e, 'tensor_copy', o, stg[s][:, :cw], r=[('stg', s)], w=[wk])
                        else:
                            g = pv[:, gcol + kk:gcol + kk + 1]
                            if e == 'act':
                                k.op(e, 'activation', o, stg[s][:, :cw], AF.Copy, scale=g,
                                     r=[('stg', s), 'pv'], w=[wk])
                            else:
                                k.op(e, 'tensor_scalar', o, stg[s][:, :cw], g, None, ALU.mult,
                                     r=[('stg', s), 'pv'], w=[wk])
                    yield conv

        def load_weight(stg, dst, src2d, nk, ncol, gcol, name):
            keys = []
            for conv in load_weight_gen(stg, dst, src2d, nk, ncol, gcol, name, keys):
                conv()
            k.wait_keys('pe', keys)

        for lay in range(L):
            pb = lay * PVL
            x_src = xT_in if lay == 0 else xb
            last = (lay == L - 1)

            st12 = ExitStack()
            dw = sb("dw", [128, 31, 128], BF16, st12)
            with ExitStack() as st:
                winb = sb("winb", [128, 8, DIN], BF16, st)
                stg = [sb("p1_stg%d" % i, [128, 1024], F32, st) for i in range(4)]
                win_blk = {}
                win_gen = load_weight_gen(stg, winb, w_in[lay], 8, DIN, pb + 0, "win", [], colmajor=True,
                                          blk_keys=win_blk)
                win_waited = set()

                def need_w(col):
                    blk = col // 1024
                    if blk not in win_waited:
                        win_waited.add(blk)
                        k.wait_keys('pe', win_blk[blk])
                xt = [sb("p1_xt%d" % i, [128, 8, 512], F32, st) for i in range(2)]
                sq = sb("p1_sq", [128, 8, 512], BF16, st)
                hT = sb("p1_hT", [128, 8, 512], BF16, st)
                rs_t = sb("p1_rs_t", [128, 512], F32, st)
                rstd = sb("p1_rstd", [128, 512], F32, st)
                ug = sb("p1_ug", [128, 2, 512], F32, st)
                ya = sb("p1_ya", [128, 2, 512], F32, st)
                sqa = sb("p1_sqa", [128, 2, 512], BF16, st)
                tmpf = [sb("p1_tmpf%d" % i, [128, 512], F32, st) for i in range(2)]
                ob = [sb("p1_ob%d" % i, [128, 512], BF16, st) for i in range(12)]
                vg = sb("p1_vg", [128, 4, DG], F32, st)
                vjunk = sb("p1_vjunk", [128, DG], F32, st)
                ssv = sb("p1_ssv", [128, 4], F32, st)
                rv_t = sb("p1_rvt", [128, 4], F32, st)
                rv = sb("p1_rv", [128, 4], F32, st)
                vnp = [sb("p1_vnp%d" % i, [128, 4, 128], BF16, st) for i in range(4)]
                tmpf2 = [sb("p1_tmpf2_%d" % i, [128, 512], F32, st) for i in range(2)]
                wsT_f = sb("p1_wsTf", [128, 4, 128], F32, st)
                wsT = sb("p1_wsT", [128, 4, 128], BF16, st)
                bsT = sb("p1_bsT", [128, 2, 128], F32, st)
                vgb = sb("p1_vgb", [128, DG], F32, st)

                k.dma('sp', wsT_f[:], wsT_in[lay], 'ld_m0', w=['wsTf'])
                k.dma('sp', bsT[:], bsT_in[lay], 'ld_m1', w=['bsT'])
                k.dma('sp', vgb[:], vgb_in[lay], 'ld_m2', w=['vgb'])
                k.op('pool', 'affine_select', wsT[:], wsT_f[:], [[0, 4], [1, 128]], ALU.is_ge, 0.0,
                     base=0, channel_multiplier=-1, r=['wsTf'], w=['wsT'])
                for i in range(4):
                    k.op('pool', 'memset', vnp[i][:], 0.0, w=[('vnp', i)])

                SSB, VB0, FB0 = 0, 1, 3
                ZB = [5, 6, 7]
                zn = 0
                obn = 0
                x_v = x_src.ap().rearrange("(k p) n -> p k n", p=128)
                loc_v = loc.ap()
                tile_toks = []

                def exch_out(o, kind, i, t):
                    for hh in range(2):
                        if kind < 3:
                            dst = exchA[t].ap()[2 * i + hh, kind, :, :]
                        else:
                            dst = exchB[t].ap()[2 * i + hh, kind - 3, :, :]
                        out_dma(ob[o][64 * hh:64 * hh + 64, :], dst, ('ob', o))

                def zchunk(m, tsl):
                    nonlocal zn
                    need_w(m * 128)
                    b = ZB[zn % 3]
                    zn += 1
                    for kk in range(8):
                        k.op('pe', 'matmul', banks[b][:], winb[:, kk, m * 128:(m + 1) * 128], hT[:, kk, :],
                             start=(kk == 0), stop=(kk == 7), r=[('hT', kk)], w=[('bank', b)])
                    return b

                def out_dma(src_ap, dst_ap, rk):
                    nonlocal obn
                    tile_toks.append(k.dma('sp', dst_ap, src_ap, 'st_p1_%d' % (obn % 12),
                                           r=[rk]))

                def next_ob():
                    nonlocal obn
                    obn += 1
                    return obn % 12

                pend_cc = []
                cc1 = {}

                def flush_cc():
                    for tt, toks in pend_cc:
                        for tok in toks:
                            k.wait('pool', tok)
                        cc1[tt] = k.collective("AllGather", [exchB[tt].ap().rearrange("s k q n -> (s k q) n").opt()],
                                               [g1B[tt].ap().rearrange("r s k q n -> (r s k q) n").opt()])
                    pend_cc.clear()

                def prologue(t):
                    s = t % 2
                    k.op('act', 'activation', sq[:], xt[s][:], AF.Square, r=[('xt', s)], w=['sq'])
                    if t + 1 < NT:
                        k.dma('sp', xt[1 - s][:], x_v[:, :, (t + 1) * 512:(t + 2) * 512], 'ld_x%d' % (1 - s),
                              w=[('xt', 1 - s)])
                    for kk in range(8):
                        k.op('pe', 'matmul', banks[SSB][:], ones[:], sq[:, kk, :], start=(kk == 0), stop=(kk == 7),
                             r=['ones', 'sq'], w=[('bank', SSB)])
                    rstd_from(banks[SSB][:], D, rstd[:], rs_t[:], ('bank', SSB), 'rstd', 'rs_t')
                    for kk in range(8):
                        e = 'dve' if kk % 2 == 0 else 'pool'
                        k.op(e, 'tensor_tensor', hT[:, kk, :], xt[s][:, kk, :], rstd[:], ALU.mult,
                             r=[('xt', s), 'rstd'], w=[('hT', kk)])

                k.dma('sp', xt[0][:], x_v[:, :, 0:512], 'ld_x0', w=[('xt', 0)])
                prologue(0)
                for conv in win_gen:
                    conv()
                for t in range(NT):
                    s = t % 2
                    tsl = slice(t * 512, (t + 1) * 512)
                    if t > 0:
                        prologue(t)
                    flush_cc()
                    for s_ in range(31):
                        if s_ % NT == t:
                            k.op('pool', 'tensor_scalar', dw[:, s_, :], ident[:],
                                 pv[:, pb + 28 + s_:pb + 29 + s_], None, ALU.mult, r=['ident', 'pv'],
                                 w=[('dw', s_)])
                    need_w(DG)
                    for sub in range(4):
                        vbk = VB0 + sub // 2
                        for kk in range(8):
                            k.op('pe', 'matmul', banks[vbk][:, (sub % 2) * DG:(sub % 2 + 1) * DG],
                                 hT[:, kk, sub * 128:(sub + 1) * 128], winb[:, kk, DG:2 * DG],
                                 start=(kk == 0), stop=(kk == 7), r=[('hT', kk)], w=[('bank', vbk)])
                    ub = [zchunk(0, tsl), zchunk(1, tsl)]
                    for h2 in range(2):
                        k.op('act', 'activation', vg[:, 2 * h2:2 * h2 + 2, :].rearrange("p a d -> p (a d)"),
                             banks[VB0 + h2][:], AF.Gelu, r=[('bank', VB0 + h2)], w=[('vg', h2)])
                    for i in range(2):
                        k.op('act', 'activation', ug[:, i, :], banks[ub[i]][:], AF.Gelu, r=[('bank', ub[i])],
                             w=[('ug', i)])
                    for sub in range(4):
                        k.op('act', 'activation', vjunk[:], vg[:, sub, :], AF.Square, accum_out=ssv[:, sub:sub + 1],
                             r=[('vg', sub // 2)], w=['vjunk', ('ssv', sub)])
                    k.op('act', 'activation', rv_t[:], ssv[:], AF.Ln, bias=EPS, scale=1.0 / DG,
                         r=[('ssv', 0), ('ssv', 1), ('ssv', 2), ('ssv', 3)], w=['rv_t'])
                    k.op('act', 'activation', rv[:], rv_t[:], AF.Exp, scale=-0.5, r=['rv_t'], w=['rv'])
                    gbv = vgb[:].rearrange("p (c i d) -> p c i d", c=2, i=2)
                    for sub in range(4):
                        vgv = vg[:, sub, :].rearrange("p (c i d) -> p c i d", c=2, i=2)
                        vnv = vnp[sub][:].rearrange("p (c i) (j d) -> p c i j d", i=2, j=2)
                        for i in range(2):
                            k.op('dve', 'scalar_tensor_tensor', vnv[:, :, i, i, :], vgv[:, :, i, :], rv[:, sub:sub + 1],
                                 gbv[:, :, i, :], ALU.mult, ALU.mult,
                                 r=[('vg', sub // 2), 'rv', 'vgb'], w=[('vnp', sub)])
                    def c_chunk(kind, i):
                        b = zchunk(10 + 2 * kind + i, tsl)
                        o = next_ob()
                        if kind == 0:
                            k.op('dve', 'tensor_scalar', ob[o][:], banks[b][:], 0.125, None, ALU.mult,
                                 r=[('bank', b)], w=[('ob', o)])
                        elif kind == 1:
                            k.op('act', 'copy', ob[o][:], banks[b][:], r=[('bank', b)], w=[('ob', o)])
                        else:
                            k.op('dve', 'tensor_copy', ob[o][:], banks[b][:], r=[('bank', b)], w=[('ob', o)])
                        exch_out(o, kind, i, t)
                    for kind in range(3):
                        for i in range(2):
                            c_chunk(kind, i)
                    a_toks = list(tile_toks)
                    tile_toks.clear()
                    for i in range(2):
                        b = zchunk(4 + i, tsl)
                        o = next_ob()
                        k.op('act', 'copy', ob[o][:], banks[b][:], r=[('bank', b)], w=[('ob', o)])
                        out_dma(ob[o][:], loc_v[DG + i * 128:DG + (i + 1) * 128, tsl], ('ob', o))
                    for i in range(2):
                        b = zchunk(6 + i, tsl)
                        k.op('act', 'copy', tmpf[i][:], banks[b][:], r=[('bank', b)], w=[('tmpf', i)])
                        b = zchunk(8 + i, tsl)
                        o = next_ob()
                        k.op('dve', 'tensor_tensor', ob[o][:], banks[b][:], tmpf[i][:], ALU.mult,
                             r=[('bank', b), ('tmpf', i)], w=[('ob', o)])
                        exch_out(o, 3, i, t)
                    for tok in a_toks:
                        k.wait('pool', tok)
                    k.collective("AllGather", [exchA[t].ap().rearrange("s k q n -> (s k q) n").opt()],
                                 [g1A[t].ap().rearrange("r s k q n -> (r s k q) n").opt()])
                    for sub in range(4):
                        for c in range(2):
                            fb = FB0 + c
                            for i in range(2):
                                k.op('pe', 'matmul', banks[fb][:, sub * 128:(sub + 1) * 128],
                                     vnp[sub][:, 2 * c + i, :], wsT[:, 2 * c + i, :],
                                     start=(i == 0), stop=(i == 1), r=[('vnp', sub), 'wsT'], w=[('bank', fb)])
                    for c in range(2):
                        fb = FB0 + c
                        tf = tmpf2[c]
                        k.op('dve', 'tensor_tensor', tf[:].rearrange("p (a t) -> p a t", a=4),
                             banks[fb][:].rearrange("p (a t) -> p a t", a=4),
                             bsT[:, c, :].unsqueeze(1).to_broadcast([128, 4, 128]), ALU.add,
                             r=[('bank', fb), 'bsT'], w=[('tmpf2', c)])
                        k.op('pool', 'tensor_tensor', ya[:, c, :], tf[:], ug[:, c, :], ALU.mult,
                             r=[('tmpf2', c), ('ug', c)], w=[('ya', c)])
                    k.op('act', 'activation', sqa[:], ya[:], AF.Square, r=[('ya', 0), ('ya', 1)], w=['sqa'])
                    for c in range(2):
                        k.op('pe', 'matmul', banks[SSB][:], ones[:], sqa[:, c, :], start=(c == 0), stop=(c == 1),
                             r=['ones', 'sqa'], w=[('bank', SSB)])
                    rstd_from(banks[SSB][:], DG, rstd[:], rs_t[:], ('bank', SSB), 'rstd', 'rs_t')
                    for c in range(2):
                        o = next_ob()
                        k.op('dve' if c == 0 else 'pool', 'tensor_tensor', ob[o][:], ya[:, c, :], rstd[:], ALU.mult,
                             r=[('ya', c), 'rstd'], w=[('ob', o)])
                        out_dma(ob[o][:], loc_v[c * 128:(c + 1) * 128, tsl], ('ob', o))
                    for i in range(2):
                        b = zchunk(18 + i, tsl)
                        k.op('act', 'activation', tmpf[i][:], banks[b][:], AF.Sigmoid, r=[('bank', b)],
                             w=[('tmpf', i)])
                        b = zchunk(16 + i, tsl)
                        o = next_ob()
                        k.op('dve', 'tensor_tensor', ob[o][:], banks[b][:], tmpf[i][:], ALU.mult,
                             r=[('bank', b), ('tmpf', i)], w=[('ob', o)])
                        exch_out(o, 4, i, t)
                    pend_cc.append((t, list(tile_toks)))
                    tile_toks.clear()
                flush_cc()
                k.barrier(skip=('cc',))


            with ExitStack() as st:
                QKV = sb("QKV", [64, 3, S], BF16, st)
                CV = sb("CV", [128, 32 + S], BF16, st)
                Vtok = sb("Vtok", [128, NKB, 64], BF16, st)
                negtri = sb("negtri", [128, 128], BF16, st)
                mones = sb("mones", [128, 128], BF16, st)
                zer = sb("zer", [128, 512], BF16, st)
                mask = sb("mask", [128, 4, 512], BF16, st)
                Eb = [sb("Eb%d" % i, [128, 1024], F32, st) for i in range(2)]
                Lb = [sb("Lb%d" % i, [128, 1024], BF16, st) for i in range(3)]
                Wb = [sb("Wb%d" % i, [128, 1024], F32, st) for i in range(2)]
                Ab = [sb("Ab%d" % i, [128, 1024], BF16, st) for i in range(3)]
                r_sb = sb("r_sb", [128, 512], F32, st)
                osb = [sb("osb%d" % i, [64, 512], BF16, st) for i in range(2)]
                csb = [sb("csb%d" % i, [128, 512], BF16, st) for i in range(2)]

                setup = []
                for t in range(NT):
                    k.wait('sp', cc1[t])
                    for rr in range(4):
                        c0 = rr * TPC + t * 512
                        k.dma('sp', QKV[:, :, c0:c0 + 512],
                              g1A[t].ap()[rr, bass.ds(slot_sp, 1), :, :, :].rearrange("s k q n -> q (s k) n"),
                              'ld_p2_%d' % t, w=[('p2ldA', rr, t)])
                        k.dma('sp', CV[:, 32 + c0:32 + c0 + 512],
                              g1B[t].ap()[rr, bass.ds(slot_sp, 1), :, :, :].rearrange("s k q n -> (s k q) n"),
                              'ld_p2_%d' % t, w=[('p2ldB', rr, t)])
                setup += ['CVpad', 'negtri']
                k.op('pool', 'memset', CV[:, 0:32], 0.0, w=['CVpad'])
                k.op('pool', 'memset', mones[:], -1.0, w=['mones'])
                k.op('pool', 'memset', zer[:], 0.0, w=['zer'])
                k.op('pool', 'affine_select', negtri[:], mones[:], [[-1, 128]], ALU.is_ge, 0.0,
                     base=0, channel_multiplier=1, r=['mones'], w=['negtri'])
                for o in range(4):
                    k.op('pool', 'affine_select', mask[:, o, :], zer[:], [[1, 512]], ALU.is_gt, NEG,
                         base=-128 * o, channel_multiplier=-1, r=['zer'], w=[('mask', o)])
                    setup.append(('mask', o))

                CB = 4
                OB = [5, 6]
                XB = 7
                vtp = banks[XB][:].bitcast(BF16)
                k.wait_keys('pe', setup)

                def prep_tile(qt):
                    rr_, t_ = divmod(qt, NT)
                    k.wait_keys('pe', [('p2ldA', rr_, t_), ('p2ldB', rr_, t_)])
                    tb = OB[qt % 2]
                    vt_ = banks[tb][:].bitcast(BF16)
                    for bi in range(4):
                        k.op('pe', 'transpose', vt_[:, bi * 64:(bi + 1) * 64],
                             QKV[:, 2, (4 * qt + bi) * 128:(4 * qt + bi + 1) * 128], ident[0:64, 0:64],
                             r=['ident'], w=[('bank', tb)])
                    k.op('dve', 'tensor_copy', Vtok[:, 4 * qt:4 * qt + 4, :].rearrange("p a d -> p (a d)"),
                         vt_[:, 0:256], r=[('bank', tb)], w=[('Vtok', qt)])
                x2_toks = []
                cc2 = {}
                pairs = []
                for qt in range(NQT):
                    npair = 2 * qt + 2
                    for p in range(npair):
                        kb0 = 4 * qt + 3 - 2 * p
                        pairs.append((qt, kb0, p, npair))
                NP = len(pairs)

                def zbank(g, j):
                    return pp[g % 2][:, j * 512:(j + 1) * 512]

                def s1(g):
                    qt, kb0, p, npair = pairs[g]
                    if p == 0:
                        prep_tile(qt)
                    for j in range(2):
                        kb = kb0 - j
                        o = kb - 4 * qt
                        k.op('pe', 'matmul', zbank(g, j), QKV[:, 1, kb * 128:(kb + 1) * 128],
                             QKV[:, 0, qt * 512:(qt + 1) * 512], start=True, stop=(o < 0), w=[('zp', g % 2)])
                        if o >= 0:
                            k.op('pe', 'matmul', zbank(g, j), ident[:], mask[:, o, :], start=False, stop=True,
                                 w=[('zp', g % 2)])

                def s2(g):
                    k.op('act', 'activation', Eb[g % 2][:], pp[g % 2][:], AF.Exp, r=[('zp', g % 2)], w=[('E', g % 2)],
                         same_ok=True)
                    k.op('act', 'activation', Lb[g % 3][:], Eb[g % 2][:], AF.Ln, bias=1.0, scale=1.0,
                         r=[('E', g % 2)], w=[('L', g % 3)], same_ok=True)

                def s3(g):
                    qt, kb0, p, npair = pairs[g]
                    L0 = Lb[g % 3][:, 0:512]
                    L1 = Lb[g % 3][:, 512:1024]
                    k.op('pe', 'matmul', zbank(g, 0), negtri[:], L0, start=False, stop=True,
                         r=[('L', g % 3)], w=[('zp', g % 2)])
                    k.op('pe', 'matmul', zbank(g, 1), negtri[:], L1, start=False, stop=False,
                         r=[('L', g % 3)], w=[('zp', g % 2)])
                    k.op('pe', 'matmul', zbank(g, 1), mones[:], L0, start=False, stop=True,
                         r=[('L', g % 3)], w=[('zp', g % 2)])
                    if p < npair - 1:
                        k.op('pe', 'matmul', banks[CB][:], ones[:], L0, start=True, stop=False,
                             r=[('L', g % 3)], w=[('bank', CB)])
                        k.op('pe', 'matmul', banks[CB][:], ones[:], L1, start=False, stop=True,
                             r=[('L', g % 3)], w=[('bank', CB)])

                def s4(g):
                    qt, kb0, p, npair = pairs[g]
                    if p == 0:
                        k.op('dve', 'tensor_copy', Wb[g % 2][:], pp[g % 2][:], r=[('zp', g % 2)], w=[('W', g % 2)])
                        if npair > 1:
                            k.op('dve', 'tensor_copy', r_sb[:], banks[CB][:], r=[('bank', CB)], w=['r'])
                    else:
                        k.op('dve', 'tensor_tensor', Wb[g % 2][:].rearrange("p (a n) -> p a n", a=2),
                             pp[g % 2][:].rearrange("p (a n) -> p a n", a=2),
                             r_sb[:].unsqueeze(1).to_broadcast([128, 2, 512]), ALU.subtract,
                             r=[('zp', g % 2), 'r'], w=[('W', g % 2)], same_ok=True)
                        if p < npair - 1:
                            k.op('dve', 'tensor_tensor', r_sb[:], banks[CB][:], r_sb[:], ALU.add,
                                 r=[('bank', CB), 'r'], w=['r'], same_ok=True)

                def s5(g):
                    k.op('act', 'activation', Ab[g % 3][:], Wb[g % 2][:], AF.Exp, r=[('W', g % 2)], w=[('A', g % 3)])

                def s6(g):
                    qt, kb0, p, npair = pairs[g]
                    obk = OB[qt % 2]
                    for j in range(2):
                        k.op('pe', 'matmul', banks[obk][0:64, :], Vtok[:, kb0 - j, :], Ab[g % 3][:, j * 512:(j + 1) * 512],
                             start=(p == 0 and j == 0), stop=(p == npair - 1 and j == 1),
                             r=[('A', g % 3), ('Vtok', (kb0 - j) // 4)], w=[('bank', obk)])
                    tpp = -(-31 // npair)
                    for s_ in range(p * tpp, min(31, (p + 1) * tpp)):
                        c0 = 32 + qt * 512 - s_
                        k.op('pe', 'matmul', banks[XB][:], dw[:, s_, :], CV[:, c0:c0 + 512],
                             start=(s_ == 0), stop=(s_ == 30), w=[('bank', XB)])
                    if p == npair - 1:
                        s = qt % 2
                        k.op('dve', 'tensor_copy', osb[s][:], banks[obk][0:64, :], r=[('bank', obk)], w=[('osb', s)])
                        pc, po = qt % NT, qt // NT
                        x2_toks.append(k.dma('sp', x2b[pc].ap()[0, :, po, :], osb[s][:], 'st_o%d' % s,
                                             r=[('osb', s)]))
                        k.op('dve', 'tensor_copy', csb[s][:], banks[XB][:], r=[('bank', XB)], w=[('csb', s)])
                        x2_toks.append(k.dma('sp', x2b[pc].ap()[1:3, :, po, :].rearrange("p q n -> (p q) n"),
                                             csb[s][:], 'st_c%d' % s, r=[('csb', s)]))
                        if po == 3:
                            for tok in x2_toks:
                                k.wait('pool', tok)
                            cc2[pc] = k.collective(
                                "AllGather", [x2b[pc].ap().rearrange("p q s n -> (p q) (s n)").opt()],
                                [g2[pc].ap().rearrange("r p q s n -> (r p q) (s n)").opt()])

                for step in range(NP + 3):
                    if step < NP:
                        s1(step)
                        s2(step)
                    if 0 <= step - 1 < NP:
                        s3(step - 1)
                        s4(step - 1)
                    if 0 <= step - 2 < NP:
                        s5(step - 2)
                    if 0 <= step - 3 < NP:
                        s6(step - 3)
                k.barrier(skip=('cc',))
            if dbg == 'p2' and lay == 0:
                for t in range(NT):
                    k.dma('sp', dbg_out[t][:, :, :, :], x2b[t][:, :, :, :], 'dbg')
                k.barrier()
                return nc


            st12.close()
            st3 = ExitStack()
            wupb = sb("wupb", [128, 8, DFF], BF16, st3)
            wup_keys = []
            with ExitStack() as st:
                stg = [sb("p3_stg%d" % i, [128, 1024], F32, st) for i in range(4)]
                woutb = sb("woutb", [128, 8, D], BF16, st)
                wout_keys = []
                for conv in load_weight_gen(stg, woutb, w_out[lay], 8, D, pb + 16, "wout", wout_keys):
                    conv()
                wup_gen = load_weight_gen(stg, wupb, w_up[lay], 8, DFF, pb + 8, "wup", wup_keys)
                xt = [sb("p3_xt%d" % i, [128, 8, 512], F32, st) for i in range(2)]
                ycats = [sb("p3_ycat%d" % i, [128, 8, 512], BF16, st) for i in range(2)]
                gin = [sb("p3_gin%d" % i, [128, 6, 512], BF16, st) for i in range(2)]
                gbt = [sb("p3_gb%d" % i, [128, 2, 512], BF16, st) for i in range(2)]
                yb = sb("p3_yb", [128, 2, 512], F32, st)
                xc = sb("p3_xc", [128, 2, 512], F32, st)
                yd = sb("p3_yd", [128, 2, 512], F32, st)
                sq3 = sb("p3_sq3", [128, 3, 1024], BF16, st)
                rs_t3 = sb("p3_rs_t3", [128, 3, 512], F32, st)
                rstd3 = sb("p3_rstd3", [128, 3, 512], F32, st)
                x_v = x_src.ap().rearrange("(k p) n -> p k n", p=128)
                xa_v = xa.ap().rearrange("(k p) n -> p k n", p=128)
                loc_v = loc.ap()
                MBK = 0
                SBK = [1, 2, 3]
                WB = [4, 5, 6]
                wn = 0

                def p3_loads(t):
                    s = t % 2
                    tsl = slice(t * 512, (t + 1) * 512)
                    k.dma('sp', xt[s][:], x_v[:, :, tsl], 'ld_x%d' % s, w=[('xt', s)])
                    k.dma('sp', ycats[s][:, 0:2, :], loc_v[0:DG, tsl].rearrange("(c p) n -> p c n", p=128),
                          'ld_ya%d' % s, w=[('ycat', s, 0), ('ycat', s, 1)])
                    k.dma('sp', gbt[s][:], loc_v[DG:2 * DG, tsl].rearrange("(c p) n -> p c n", p=128),
                          'ld_gb%d' % s, w=[('gb', s)])
                    k.wait('sp', cc2[t])
                    for rr in range(4):
                        k.dma('sp', gin[s][64 * (rr % 2):64 * (rr % 2) + 64, (rr // 2)::2, :],
                              g2[t].ap()[rr, :, :, bass.ds(slot_sp, 1), :].rearrange("p q s n -> q (p s) n"),
                              'ld_g%d' % s, w=[('gin', s, rr)])

                def mul2(dst_fn, src_fn, rs_ap, rkeys, wkeys):
                    for c in range(2):
                        k.op('dve' if c == 0 else 'pool', 'tensor_tensor', dst_fn(c), src_fn(c), rs_ap, ALU.mult,
                             r=rkeys(c), w=wkeys(c))

                p3_loads(0)
                for t in range(NT):
                    s = t % 2
                    tsl = slice(t * 512, (t + 1) * 512)
                    ycat = ycats[s]
                    GIN = [('gin', s, rr) for rr in range(4)]
                    if t + 1 < NT:
                        p3_loads(t + 1)
                    quota = -(-32 // NT)
                    convs = [next(wup_gen, None) for _ in range(min(quota, len(stg)))]
                    for c in range(2):
                        k.op('pool', 'tensor_tensor', yb[:, c, :], gbt[s][:, c, :], gin[s][:, 2 + c, :], ALU.mult,
                             r=[('gb', s)] + GIN, w=[('yb', c)])
                    for c in range(2):
                        k.op('pe', 'matmul', banks[MBK][:], ones[:], gin[s][:, 4 + c, :], start=(c == 0), stop=(c == 1),
                             r=['ones'] + GIN, w=[('bank', MBK)])
                    for c in range(2):
                        k.op('dve', 'scalar_tensor_tensor', xc[:, c, :], banks[MBK][:], -1.0 / DG, gin[s][:, 4 + c, :],
                             ALU.mult, ALU.add, r=[('bank', MBK)] + GIN, w=[('xc', c)])
                    k.op('act', 'activation', sq3[:, 0, :], yb[:].rearrange("p c n -> p (c n)"), AF.Square,
                         r=[('yb', 0), ('yb', 1)], w=[('sq3', 0)])
                    k.op('act', 'activation', sq3[:, 1, :], gin[s][:, 0:2, :].rearrange("p c n -> p (c n)"), AF.Square,
                         r=GIN, w=[('sq3', 1)])
                    k.op('act', 'activation', sq3[:, 2, :], xc[:].rearrange("p c n -> p (c n)"), AF.Square,
                         r=[('xc', 0), ('xc', 1)], w=[('sq3', 2)])
                    for g_ in range(3):
                        for c in range(2):
                            k.op('pe', 'matmul', banks[SBK[g_]][:], ones[:], sq3[:, g_, c * 512:(c + 1) * 512],
                                 start=(c == 0), stop=(c == 1), r=['ones', ('sq3', g_)], w=[('bank', SBK[g_])])
                    k.op('act', 'activation', rs_t3[:, 2, :], banks[SBK[2]][:], AF.Ln, bias=EPS,
                         scale=1.0 / DG, r=[('bank', SBK[2])], w=[('rs_t3', 2)])
                    k.op('act', 'activation', rstd3[:, 2, :], rs_t3[:, 2, :], AF.Exp, scale=-0.5,
                         r=[('rs_t3', 2)], w=['rstd3d'])
                    mul2(lambda c: xc[:, c, :], lambda c: xc[:, c, :], rstd3[:, 2, :],
                         lambda c: [('xc', c), 'rstd3d'], lambda c: [('xc', c)])
                    for g_ in range(2):
                        k.op('act', 'activation', rs_t3[:, g_, :], banks[SBK[g_]][:], AF.Ln, bias=EPS,
                             scale=1.0 / DG, r=[('bank', SBK[g_])], w=[('rs_t3', g_)])
                    k.op('act', 'activation', rstd3[:, 0:2, :].rearrange("p g n -> p (g n)"),
                         rs_t3[:, 0:2, :].rearrange("p g n -> p (g n)"), AF.Exp, scale=-0.5,
                         r=[('rs_t3', 0), ('rs_t3', 1)], w=['rstd3'])
                    mul2(lambda c: ycat[:, 2 + c, :], lambda c: yb[:, c, :], rstd3[:, 0, :],
                         lambda c: [('yb', c), 'rstd3'], lambda c: [('ycat', s, 2 + c)])
                    mul2(lambda c: ycat[:, 4 + c, :], lambda c: gin[s][:, c, :], rstd3[:, 1, :],
                         lambda c: GIN + ['rstd3'], lambda c: [('ycat', s, 4 + c)])
                    for c in range(2):
                        k.op('act', 'activation', yd[:, c, :], xc[:, c, :], AF.Silu,
                             bias=pv[:, pb + 26 + c:pb + 27 + c], scale=pv[:, pb + 24 + c:pb + 25 + c],
                             r=[('xc', c), 'pv'], w=[('yd', c)])
                    k.op('act', 'activation', sq3[:, 0, :], yd[:].rearrange("p c n -> p (c n)"), AF.Square,
                         r=[('yd', 0), ('yd', 1)], w=[('sq3', 0)])
                    for c in range(2):
                        k.op('pe', 'matmul', banks[SBK[0]][:], ones[:], sq3[:, 0, c * 512:(c + 1) * 512],
                             start=(c == 0), stop=(c == 1), r=['ones', ('sq3', 0)], w=[('bank', SBK[0])])
                    k.op('act', 'activation', rs_t3[:, 0, :], banks[SBK[0]][:], AF.Ln, bias=EPS,
                         scale=1.0 / DG, r=[('bank', SBK[0])], w=[('rs_t3', 0)])
                    k.op('act', 'activation', rstd3[:, 0, :], rs_t3[:, 0, :], AF.Exp, scale=-0.5,
                         r=[('rs_t3', 0)], w=['rstd3'])
                    mul2(lambda c: ycat[:, 6 + c, :], lambda c: yd[:, c, :], rstd3[:, 0, :],
                         lambda c: [('yd', c), 'rstd3'], lambda c: [('ycat', s, 6 + c)])
                    if t == 0:
                        k.wait_keys('pe', wout_keys)
                    for m in range(8):
                        b = WB[wn % 3]
                        wn += 1
                        for kk in range(8):
                            k.op('pe', 'matmul', banks[b][:], woutb[:, kk, m * 128:(m + 1) * 128], ycat[:, kk, :],
                                 start=(kk == 0), stop=(kk == 7), r=[('ycat', s, kk)], w=[('bank', b)])
                        k.op('dve', 'tensor_tensor', xt[s][:, m, :], banks[b][:], xt[s][:, m, :], ALU.add,
                             r=[('bank', b), ('xt', s)], w=[('xt', s)])
                    k.dma('sp', xa_v[:, :, tsl], xt[s][:], 'st_x%d' % s, r=[('xt', s)])
                    for conv in convs:
                        if conv is not None:
                            conv()
                    for _ in range(quota - len(convs)):
                        conv = next(wup_gen, None)
                        if conv is not None:
                            conv()
                for conv in wup_gen:
                    conv()
                k.barrier()
            if dbg == 'p3a' and lay == 0:
                k.dma('sp', dbg_out[0][:, :], xa[:, :], 'dbg')
                k.barrier()
                return nc

            with ExitStack() as st:
                TF = 512
                NF = TPC // TF
                wdnb = sb("wdnb", [128, 32, D], BF16, st)
                k.wait_keys('pe', wup_keys)
                xt0 = sb("f_xt0", [128, 8, TF], F32, st)
                hT = sb("f_hT", [128, 8, TF], BF16, st)
                aT = sb("f_aT", [128, 32, TF], BF16, st)
                sq = aT[:, 0:8, :]
                SQW = ['sq'] + [('aT', m) for m in range(8)]
                rl = [sb("f_rl%d" % i, [128, TF], BF16, st) for i in range(2)]
                rstd = sb("f_rstd", [128, TF], F32, st)
                stB = ExitStack()
                stgB = [sb("f_stg%d" % i, [128, 1024], F32, stB) for i in range(4)]
                wdn_keys = []
                wdn_gen = load_weight_gen(stgB, wdnb, w_down[lay], 32, D, None, "wdn", wdn_keys)
                xt = [xt0, None]
                xa_v = xa.ap().rearrange("(k p) n -> p k n", p=128)
                dst = outT if last else xb
                dst_v = dst.ap().rearrange("(k p) n -> p k n", p=128)
                un = 0
                k.dma('sp', xt[0][:], xa_v[:, :, 0:TF], 'ld_x0', w=[('xt', 0)])
                for t in range(NF):
                    s = t % 2
                    tsl = slice(t * TF, (t + 1) * TF)
                    if t >= 1 and t + 1 < NF:
                        k.dma('sp', xt[1 - s][:], xa_v[:, :, (t + 1) * TF:(t + 2) * TF], 'ld_x%d' % (1 - s),
                              w=[('xt', 1 - s)])
                    k.op('act', 'activation', sq, xt[s][:], AF.Square, r=[('xt', s)], w=SQW)
                    for kk in range(8):
                        k.op('pe', 'matmul', banks[0][:, 0:TF], ones[:], sq[:, kk, :], start=(kk == 0), stop=(kk == 7),
                             r=['ones', 'sq'], w=[('bank', 0)])
                    rstd_from(banks[0][:, 0:TF], D, rstd[:], rstd[:], ('bank', 0), 'rstd', 'rstd')
                    for kk in range(8):
                        k.op('dve' if kk % 2 == 0 else 'pool', 'tensor_tensor', hT[:, kk, :], xt[s][:, kk, :], rstd[:],
                             ALU.mult, r=[('xt', s), 'rstd'], w=[('hT', kk)])
                    for m in range(32):
                        hb = un % 6
                        un += 1
                        pso = banks[1 + hb][:, 0:TF]
                        for kk in range(8):
                            k.op('pe', 'matmul', pso, wupb[:, kk, m * 128:(m + 1) * 128], hT[:, kk, :],
                                 start=(kk == 0), stop=(kk == 7), r=[('hT', kk)], w=[('bank', 1 + hb)])
                        ri = m % 2
                        if m % 2 == 0:
                            k.op('act', 'activation', rl[ri][:], pso, AF.Relu, r=[('bank', 1 + hb)], w=[('rl', ri)])
                        else:
                            k.op('dve', 'tensor_scalar', rl[ri][:], pso, 0.0, None, ALU.max,
                                 r=[('bank', 1 + hb)], w=[('rl', ri)])
                        k.op('pool', 'tensor_tensor', aT[:, m, :], rl[ri][:], rl[ri][:], ALU.mult,
                             r=[('rl', ri)], w=([('aT', m), 'sq'] if m < 8 else [('aT', m)]))
                        if t == 0:
                            conv = next(wdn_gen, None)
                            if conv is not None:
                                conv()
                    if t == 0:
                        for conv in wdn_gen:
                            conv()
                        k.wait_keys('pe', wdn_keys)
                        if NF > 1:
                            k.wait_keys('sp', wdn_keys)
                            stB.close()
                            xt[1] = sb("f_xt1", [128, 8, TF], F32, st)
                            k.dma('sp', xt[1][:], xa_v[:, :, TF:2 * TF], 'ld_x1', w=[('xt', 1)])
                    for m in range(8):
                        hb = un % 6
                        un += 1
                        pso = banks[1 + hb][:, 0:TF]
                        for kk in range(32):
                            k.op('pe', 'matmul', pso, wdnb[:, kk, m * 128:(m + 1) * 128], aT[:, kk, :],
                                 start=(kk == 0), stop=(kk == 31), r=[('aT', kk)], w=[('bank', 1 + hb)])
                        k.op('dve', 'tensor_tensor', xt[s][:, m, :], pso, xt[s][:, m, :], ALU.add,
                             r=[('bank', 1 + hb), ('xt', s)], w=[('xt', s)])
                    if last:
                        k.op('act', 'activation', sq, xt[s][:], AF.Square, r=[('xt', s)], w=SQW)
                        for kk in range(8):
                            k.op('pe', 'matmul', banks[7][:, 0:TF], ones[:], sq[:, kk, :], start=(kk == 0),
                                 stop=(kk == 7), r=['ones', 'sq'], w=[('bank', 7)])
                        rstd_from(banks[7][:, 0:TF], D, rstd[:], rstd[:], ('bank', 7), 'rstd', 'rstd')
                        for kk in range(8):
                            k.op('dve', 'scalar_tensor_tensor', xt[s][:, kk, :], xt[s][:, kk, :],
                                 pv[:, PVL * L + kk:PVL * L + kk + 1], rstd[:], ALU.mult, ALU.mult,
                                 r=[('xt', s), 'pv', 'rstd'], w=[('xt', s)])
                    k.dma('sp', dst_v[:, :, tsl], xt[s][:], 'st_x%d' % s, r=[('xt', s)])
                k.barrier()
                if NF == 1:
                    stB.close()
            st3.close()
    return nc


def host_inputs(S, L, x, norm_mix_g, w_in, gmlp_v_g, gmlp_w_s, gmlp_b_s, short_conv_w, conf_conv_w,
                conf_ln_g, conf_ln_b, mix_out_g, w_out, norm_ffn_g, w_up, w_down, final_norm_g):
    TPC = S // 4
    f = lambda a: np.ascontiguousarray(np.asarray(a, dtype=np.float32))
    x = f(x)
    cols = lambda v: f(v).reshape(-1, 128).T
    vgb = np.ascontiguousarray(np.broadcast_to(f(gmlp_v_g)[:, None, :], (L, 128, DG)))
    wsT = np.ascontiguousarray(f(gmlp_w_s).transpose(0, 3, 1, 2))
    bs = f(gmlp_b_s)
    bsT = np.ascontiguousarray(np.repeat(bs.reshape(L, 2, 2, 1, 128), 64, axis=3).reshape(L, 2, 128, 128)
                               .transpose(0, 2, 1, 3))
    scw = f(short_conv_w)
    ccw = f(conf_conv_w)
    maps = []
    for c in range(8):
        b, j = c // 4, c % 4
        pvs = []
        for l in range(L):
            cw = np.zeros((128, 31), np.float32)
            for s_ in range(31):
                cw[64:, s_] = ccw[l, 30 - s_, 64 * j:64 * j + 64]
                if s_ <= 2:
                    cw[:64, s_] = scw[l, 2 - s_, 64 * j:64 * j + 64]
            pvs += [cols(norm_mix_g[l]), cols(norm_ffn_g[l]), cols(mix_out_g[l]), cols(conf_ln_g[l]),
                    cols(conf_ln_b[l]), cw]
        pvs.append(cols(final_norm_g))
        pv = np.ascontiguousarray(np.concatenate(pvs, axis=1))
        maps.append({
            "xT": np.ascontiguousarray(x[b, j * TPC:(j + 1) * TPC, :].T),
            "w_in": f(w_in), "w_out": f(w_out), "w_up": f(w_up), "w_down": f(w_down),
            "pv": pv, "vgb": vgb, "wsT": wsT, "bsT": bsT,
        })
    return maps


_NC_CACHE = {}


def run(S, L, inputs, dbg=None):
    key = (S, L, dbg)
    if key not in _NC_CACHE:
        _NC_CACHE[key] = build(S, L, dbg)
    nc = _NC_CACHE[key]
    maps = host_inputs(S, L, **inputs)
    res = run_bass_kernel_spmd(nc, maps, core_ids=list(range(8)))
    return res.results


def kernel(**inputs):
    x = inputs["x"]
    B, S, _ = x.shape
    L = inputs["w_in"].shape[0]
    TPC = S // 4
    results = run(S, L, inputs)
    out = np.empty((B, S, D), np.float32)
    for c in range(8):
        b, j = c // 4, c % 4
        out[b, j * TPC:(j + 1) * TPC, :] = results[c]["outT"].T
    return out
```
